# Optimizing a Trainium2 kernel written in Bass

```python
import jax, jax.numpy as jnp
from jax import lax
import numpy as np

D_MODEL = 1024
BATCH = 8
SEQ = 2048
DEPTH = 1

D_FF = 4 * D_MODEL
GMLP_WIDTH = D_MODEL
GMLP_HEADS = 8
GMLP_HEAD_DIM = GMLP_WIDTH // GMLP_HEADS
CHUNK = 128
CONV_WIDTH = D_MODEL
CONV_KERNEL = 31
N_MOD = 9
EPS = 1e-6
SPLIT_SIZES = (GMLP_WIDTH, GMLP_WIDTH, CONV_WIDTH, CONV_WIDTH, D_MODEL, D_MODEL)
D_IN = sum(SPLIT_SIZES)

kernel_name = "hybrid_gmlp_conformer_conv_gated_macaron"


def _split_points(sizes):
    pts, acc = [], 0
    for s in sizes[:-1]:
        acc += s
        pts.append(acc)
    return pts


def rms_norm(x, g):
    xf = x.astype(jnp.float32)
    y = xf * lax.rsqrt(jnp.mean(xf * xf, axis=-1, keepdims=True) + EPS)
    return (y * g.astype(jnp.float32)).astype(x.dtype)


def layer_norm(x, g, b):
    xf = x.astype(jnp.float32)
    mu = jnp.mean(xf, axis=-1, keepdims=True)
    var = jnp.mean(jnp.square(xf - mu), axis=-1, keepdims=True)
    y = (xf - mu) * lax.rsqrt(var + EPS)
    return (y * g.astype(jnp.float32) + b.astype(jnp.float32)).astype(x.dtype)


def modulate(h, shift, scale):
    return h * (1.0 + scale[:, None, :]) + shift[:, None, :]


def swiglu(h, w_gate, w_up, w_down):
    return (jax.nn.silu(h @ w_gate) * (h @ w_up)) @ w_down


def token_mixer(h, w_in, b_in, sgu_ln_g, sgu_ln_b, sgu_w_s, sgu_b_s,
                conv_w, conv_b, conv_ln_g, conv_ln_b, w_branch_a, w_branch_b, w_out):
    bsz, seq, _ = h.shape
    proj = h @ w_in + b_in
    u, v, cv, cg, ga, gb = jnp.split(proj, _split_points(SPLIT_SIZES), axis=-1)

    u = jax.nn.gelu(u)
    v = layer_norm(jax.nn.gelu(v), sgu_ln_g, sgu_ln_b)
    n_chunks = seq // CHUNK
    v = v.reshape(bsz, n_chunks, CHUNK, GMLP_HEADS, GMLP_HEAD_DIM)
    causal = jnp.tril(jnp.ones((CHUNK, CHUNK), dtype=bool))
    w_s = jnp.where(causal[None], sgu_w_s, jnp.zeros_like(sgu_w_s))
    v_mix = jnp.einsum('hts,bcshd->bcthd', w_s, v) + sgu_b_s.T[None, None, :, :, None]
    y_a = (u * v_mix.reshape(bsz, seq, GMLP_WIDTH)) @ w_branch_a

    z = cv * jax.nn.sigmoid(cg)
    z = lax.conv_general_dilated(
        z, conv_w[:, None, :], window_strides=(1,),
        padding=[(CONV_KERNEL - 1, 0)],
        dimension_numbers=('NWC', 'WIO', 'NWC'),
        feature_group_count=CONV_WIDTH) + conv_b
    z = jax.nn.silu(layer_norm(z, conv_ln_g, conv_ln_b))
    y_b = z @ w_branch_b

    merged = jax.nn.sigmoid(ga) * y_a + jax.nn.sigmoid(gb) * y_b
    return merged @ w_out


def setup_inputs(seed: int = 0) -> dict:
    key = jax.random.key(seed)
    ks = jax.random.split(key, 32)
    f32 = jnp.float32

    def nrm(k, shape, fan_in, mult=1.0):
        return jax.random.normal(k, shape, f32) * (mult * fan_in ** -0.5)

    def gain(k, shape):
        return 1.0 + 0.05 * jax.random.normal(k, shape, f32)

    def bias(k, shape):
        return 0.02 * jax.random.normal(k, shape, f32)

    L = DEPTH
    return {
        "x": jax.random.normal(ks[0], (BATCH, SEQ, D_MODEL), f32),
        "c": jax.random.normal(ks[1], (BATCH, D_MODEL), f32),
        "ada_w": nrm(ks[2], (L, D_MODEL, N_MOD * D_MODEL), D_MODEL, 0.5),
        "ada_b": bias(ks[3], (L, N_MOD * D_MODEL)),
        "norm_ffn1": gain(ks[4], (L, D_MODEL)),
        "ffn1_w_gate": nrm(ks[5], (L, D_MODEL, D_FF), D_MODEL),
        "ffn1_w_up": nrm(ks[6], (L, D_MODEL, D_FF), D_MODEL),
        "ffn1_w_down": nrm(ks[7], (L, D_FF, D_MODEL), D_FF),
        "norm_mix": gain(ks[8], (L, D_MODEL)),
        "mix_w_in": nrm(ks[9], (L, D_MODEL, D_IN), D_MODEL),
        "mix_b_in": bias(ks[10], (L, D_IN)),
        "sgu_ln_g": gain(ks[11], (L, GMLP_WIDTH)),
        "sgu_ln_b": bias(ks[12], (L, GMLP_WIDTH)),
        "sgu_w_s": nrm(ks[13], (L, GMLP_HEADS, CHUNK, CHUNK), CHUNK),
        "sgu_b_s": gain(ks[14], (L, GMLP_HEADS, CHUNK)),
        "conv_w": nrm(ks[15], (L, CONV_KERNEL, CONV_WIDTH), CONV_KERNEL),
        "conv_b": bias(ks[16], (L, CONV_WIDTH)),
        "conv_ln_g": gain(ks[17], (L, CONV_WIDTH)),
        "conv_ln_b": bias(ks[18], (L, CONV_WIDTH)),
        "w_branch_a": nrm(ks[19], (L, GMLP_WIDTH, D_MODEL), GMLP_WIDTH),
        "w_branch_b": nrm(ks[20], (L, CONV_WIDTH, D_MODEL), CONV_WIDTH),
        "w_out": nrm(ks[21], (L, D_MODEL, D_MODEL), D_MODEL),
        "norm_ffn2": gain(ks[22], (L, D_MODEL)),
        "ffn2_w_gate": nrm(ks[23], (L, D_MODEL, D_FF), D_MODEL),
        "ffn2_w_up": nrm(ks[24], (L, D_MODEL, D_FF), D_MODEL),
        "ffn2_w_down": nrm(ks[25], (L, D_FF, D_MODEL), D_FF),
        "norm_final": gain(ks[26], (D_MODEL,)),
    }


def reference(x, c, ada_w, ada_b, norm_ffn1, ffn1_w_gate, ffn1_w_up, ffn1_w_down,
              norm_mix, mix_w_in, mix_b_in, sgu_ln_g, sgu_ln_b, sgu_w_s, sgu_b_s,
              conv_w, conv_b, conv_ln_g, conv_ln_b, w_branch_a, w_branch_b, w_out,
              norm_ffn2, ffn2_w_gate, ffn2_w_up, ffn2_w_down, norm_final):
    c_act = jax.nn.silu(c)
    for l in range(DEPTH):
        mod = c_act @ ada_w[l] + ada_b[l]
        sh1, sc1, g1, sh2, sc2, g2, sh3, sc3, g3 = jnp.split(mod, N_MOD, axis=-1)

        h = modulate(rms_norm(x, norm_ffn1[l]), sh1, sc1)
        x = x + 0.5 * g1[:, None, :] * swiglu(h, ffn1_w_gate[l], ffn1_w_up[l], ffn1_w_down[l])

        h = modulate(rms_norm(x, norm_mix[l]), sh2, sc2)
        y = token_mixer(h, mix_w_in[l], mix_b_in[l], sgu_ln_g[l], sgu_ln_b[l],
                        sgu_w_s[l], sgu_b_s[l], conv_w[l], conv_b[l], conv_ln_g[l],
                        conv_ln_b[l], w_branch_a[l], w_branch_b[l], w_out[l])
        x = x + g2[:, None, :] * y

        h = modulate(rms_norm(x, norm_ffn2[l]), sh3, sc3)
        x = x + 0.5 * g3[:, None, :] * swiglu(h, ffn2_w_gate[l], ffn2_w_up[l], ffn2_w_down[l])

    return rms_norm(x, norm_final)
```

```python
import os
from contextlib import ExitStack

import numpy as np
import concourse.bass as bass
import concourse.mybir as mybir
from concourse.bass_utils import run_bass_kernel_spmd

F32 = mybir.dt.float32
BF16 = mybir.dt.bfloat16
AF = mybir.ActivationFunctionType
ALU = mybir.AluOpType

EPS = 1e-6
NS = 6
NV = 440
STAGE = int(os.environ.get("KSTAGE", "9"))

C_NF1, C_NMIX, C_NF2, C_NFIN = 0, 8, 16, 24
C_BIN = 32
C_SLG, C_SLB = 80, 88
C_CVB, C_CLG, C_CLB = 96, 104, 112
C_ADAB = 120
C_CW = 192


class Tile:
    __slots__ = ("w", "r")

    def __init__(self):
        self.w = {}
        self.r = {}


def alias_into(new_tiles, old_tiles):
    w, r = {}, {}
    for t in old_tiles:
        for k, v in t.w.items():
            if w.get(k, 0) < v:
                w[k] = v
        for k, v in t.r.items():
            if r.get(k, 0) < v:
                r[k] = v
    for t in new_tiles:
        t.w = dict(w)
        t.r = dict(r)


class Sched:
    ENG = ("pe", "act", "dve", "pool", "sp")

    def __init__(self):
        self.q = {e: [] for e in self.ENG}
        self.seq = {e: 0 for e in self.ENG}
        self.known = {e: {} for e in self.ENG}
        self.dsem = {}

    def emit(self, eng, fn, reads=(), writes=(), dma=None):
        raw, oth = {}, {}
        for t in reads:
            for k, v in t.w.items():
                if raw.get(k, 0) < v:
                    raw[k] = v
        for t in writes:
            for k, v in t.w.items():
                if oth.get(k, 0) < v:
                    oth[k] = v
            for k, v in t.r.items():
                if oth.get(k, 0) < v:
                    oth[k] = v
        waits = []
        kn = self.known[eng]
        for k, v in raw.items():
            if k == eng and eng == "pe":
                continue
            if kn.get(k, 0) >= v:
                continue
            kn[k] = v
            waits.append((k, v))
        for k, v in oth.items():
            if k == eng:
                continue
            if kn.get(k, 0) >= v:
                continue
            kn[k] = v
            waits.append((k, v))
        if dma is None:
            self.seq[eng] += 1
            tok = (eng, self.seq[eng])
        else:
            self.dsem[dma] = self.dsem.get(dma, 0) + 16
            tok = (dma, self.dsem[dma])
        self.q[eng].append((fn, waits, tok))
        for t in reads:
            if t.r.get(tok[0], 0) < tok[1]:
                t.r[tok[0]] = tok[1]
        for t in writes:
            t.w = {tok[0]: tok[1]}
            t.r = {}
        return tok


class Banks:
    def __init__(self):
        self.next = 0
        self.held = set()

    def take(self, hold=False):
        for _ in range(16):
            b = self.next
            self.next = (self.next + 1) % 8
            if b not in self.held:
                if hold:
                    self.held.add(b)
                return b
        raise RuntimeError("no psum bank")

    def drop(self, b):
        self.held.discard(b)


class Buf:
    def __init__(self, aps):
        self.aps = aps
        self.tiles = [Tile() for _ in aps]
        self.i = 0

    def next(self):
        j = self.i % len(self.aps)
        self.i += 1
        return self.aps[j], self.tiles[j]


def build_program(stage=STAGE):
    nc = bass.Bass("TRN2", target_bir_lowering=False)
    S = Sched()
    PB = Banks()
    es = ExitStack()

    def din(name, shape):
        return nc.dram_tensor(name, list(shape), F32, kind="ExternalInput").ap()

    d_xT = din("xT", [1024, 2048])
    d_cvec = din("cvec", [128, 8])
    d_vecs = din("vecs", [128, NV])
    d_wsT = din("wsT", [128, 8, 128])
    d_mask = din("mask", [128, 128])
    d_ident = din("ident", [128, 128])
    d_rows = din("rows", [1, 3072])
    d_ada = din("ada_w", [1024, 9216])
    d_f = {}
    for f in ("f1", "f2"):
        d_f[f + "g"] = din(f + "g", [1024, 4096])
        d_f[f + "u"] = din(f + "u", [1024, 4096])
        d_f[f + "d"] = din(f + "d", [4096, 1024])
    d_win = din("w_in", [1024, 6144])
    d_wa = din("w_a", [1024, 1024])
    d_wb = din("w_b", [1024, 1024])
    d_wo = din("w_o", [1024, 1024])
    d_out = nc.dram_tensor("outT", [1024, 2048], F32, kind="ExternalOutput").ap()

    def sb(name, shape, dt):
        return es.enter_context(nc.sbuf_tensor(name, list(shape), dt))

    X = sb("X", [128, 8, 2048], F32)
    H = sb("H", [128, 8, 2048], BF16)
    RING = sb("RING", [128, NS, 4, 2, 512], BF16)
    VECS = sb("VECS", [128, NV], F32)
    MOD = sb("MOD", [128, 72], F32)
    DER = sb("DER", [128, 40], F32)
    CVEC = sb("CVEC", [128, 8], F32)
    CACT = sb("CACT", [128, 8], BF16)
    ONESB = sb("ONESB", [128, 128], BF16)
    IDB = sb("IDB", [128, 128], BF16)
    WSB = sb("WSB", [128, 8, 128], BF16)
    CH = sb("CH", [128, 8, 128], F32)
    BVROW = sb("BVROW", [1, 1024], BF16)
    ONEROWB = sb("ONEROWB", [1, 128], BF16)
    CWB = sb("CWB", [128, 248], BF16)
    HALO = sb("HALO", [128, 8, 32], BF16)
    RTb = sb("RTb", [128, 2, 512], F32)
    SQb = sb("SQb", [128, 4, 512], BF16)
    TMPNb = sb("TMPNb", [128, 2, 512], F32)
    STATS = sb("STATS", [128, 2, 12], F32)
    MVT = sb("MVT", [128, 2, 4], F32)
    EPSC = sb("EPSC", [128, 8], F32)
    SCRN = 18688
    SCR = sb("SCR", [128, SCRN], BF16)

    PS = [es.enter_context(nc.psum_tensor("ps%d" % i, [128, 512], F32)) for i in range(8)]
    PT = [Tile() for _ in range(8)]

    sem_eng = {e: es.enter_context(nc.semaphore("s_" + e)) for e in ("pe", "act", "dve", "pool")}
    dma_keys = ["c", "out"] + ["x%d" % i for i in range(4)] + ["w%d" % i for i in range(NS)]
    sem_dma = {k: es.enter_context(nc.semaphore("d_" + k)) for k in dma_keys}

    def MM(out, lhsT, rhs, start, stop, reads, writes):
        S.emit("pe", lambda e, o=out, l=lhsT, r=rhs, a=start, b=stop:
               e.matmul(o, lhsT=l, rhs=r, start=a, stop=b), reads, writes)

    def ACT(out, in_, func, reads, writes, bias=None, scale=None):
        kw = {}
        if bias is not None:
            kw["bias"] = bias
        if scale is not None:
            kw["scale"] = scale
        S.emit("act", lambda e, o=out, i=in_, f=func, kw=kw:
               e.activation(out=o, in_=i, func=f, **kw), reads, writes)

    def TT(out, in0, in1, op, reads, writes, eng="dve"):
        S.emit(eng, lambda e, o=out, a=in0, b=in1, p=op:
               e.tensor_tensor(out=o, in0=a, in1=b, op=p), reads, writes)

    def TS(out, in0, s1, s2, op0, op1, reads, writes, eng="dve"):
        if op1 is None:
            S.emit(eng, lambda e, o=out, a=in0, x=s1, p=op0:
                   e.tensor_scalar(out=o, in0=a, scalar1=x, scalar2=None, op0=p), reads, writes)
        else:
            S.emit(eng, lambda e, o=out, a=in0, x=s1, y=s2, p=op0, q=op1:
                   e.tensor_scalar(out=o, in0=a, scalar1=x, scalar2=y, op0=p, op1=q), reads, writes)

    def STT(out, in0, scalar, in1, op0, op1, reads, writes):
        S.emit("dve", lambda e, o=out, a=in0, s=scalar, b=in1, p=op0, q=op1:
               e.scalar_tensor_tensor(out=o, in0=a, scalar=s, in1=b, op0=p, op1=q), reads, writes)

    def RSQRT(out, in_, reads, wtile):
        ACT(out, in_, AF.Sqrt, reads, [wtile], bias=EPSC[:in_.shape[0], 0:1])
        S.emit("dve", lambda e, o=out: e.reciprocal(out=o, in_=o), [wtile], [wtile])

    def CP(out, in_, reads, writes, eng="dve"):
        S.emit(eng, lambda e, o=out, i=in_: e.tensor_copy(out=o, in_=i), reads, writes)

    def MSET(ap, val, writes, eng="dve"):
        S.emit(eng, lambda e, a=ap, v=val: e.memset(a, v), (), writes)

    def DMA(q, out, in_, key, reads, writes):
        S.emit(q, lambda e, o=out, i=in_: e.dma_start(out=o, in_=i), reads, writes, dma=key)

    XT = [[Tile() for _ in range(4)] for _ in range(8)]
    HT = [[Tile() for _ in range(4)] for _ in range(8)]
    RTile = [Tile() for _ in range(NS)]
    T_VECS, T_CVEC, T_CACT, T_ONESB, T_IDB, T_WSB, T_CH = (Tile() for _ in range(7))
    T_BVROW, T_ONEROWB, T_CWB, T_HALO, T_EPSC = (Tile() for _ in range(5))
    MODT = [Tile() for _ in range(3)]
    DERT = [Tile() for _ in range(3)]
    RT = Buf([RTb[:, i, :] for i in range(2)])
    SQ = Buf([SQb[:, i, :] for i in range(4)])
    TMPN = Buf([TMPNb[:, i, :] for i in range(2)])
    STATSB = Buf([STATS[:, i, :] for i in range(2)])
    MVTB = Buf([MVT[:, i, :] for i in range(2)])

    def Xap(k, tt):
        return X[:, k, tt * 512:(tt + 1) * 512]

    def Hap(k, tt):
        return H[:, k, tt * 512:(tt + 1) * 512]

    class Scr:
        def __init__(self):
            self.off = 0

        def take(self, nbytes, dt, parts=None):
            assert self.off % 32 == 0
            a = self.off // 2
            n2 = nbytes // 2
            assert a + n2 <= SCRN, (a, n2)
            v = SCR[:, a:a + n2] if parts is None else SCR[0:parts, a:a + n2]
            self.off += (nbytes + 31) // 32 * 32
            if dt is F32:
                v = v.bitcast(F32)
            return v

    def colblock(Wd_, c0):
        return Wd_.rearrange("(a b p) n -> p a b n", a=4, b=2, p=128)[:, :, :, c0:c0 + 512]

    def rowblock(Wd_, j):
        return Wd_.rearrange("(s p) (c n) -> p s c n", p=128, c=2)[:, 4 * j:4 * j + 4]

    blocks = []

    def add_ada(g):
        for blk in range(6):
            blocks.append(("ada%d_%d" % (g, blk), colblock(d_ada, 3072 * g + 512 * blk)))

    def add_ffn(f):
        def gu(j):
            blocks.append(("%sg%d" % (f, j), colblock(d_f[f + "g"], 512 * j)))
            blocks.append(("%su%d" % (f, j), colblock(d_f[f + "u"], 512 * j)))

        def dn(j):
            blocks.append(("%sd%d" % (f, j), rowblock(d_f[f + "d"], j)))
        gu(0)
        for j in range(1, 8):
            gu(j)
            dn(j - 1)
        dn(7)

    def add_mixer(hh):
        def inb(i):
            blocks.append(("m%din%d" % (hh, i), colblock(d_win, 512 * i)))
        for i in (0, 1, 2, 3):
            inb(i)
        for g in range(2):
            inb(8 + g)
            blocks.append(("m%dwa%d" % (hh, g), colblock(d_wa, 512 * g)))
        for g in range(2):
            inb(4 + g)
            inb(6 + g)
        for g in range(2):
            inb(10 + g)
            blocks.append(("m%dwb%d" % (hh, g), colblock(d_wb, 512 * g)))
        for g in range(2):
            blocks.append(("m%dwo%d" % (hh, g), colblock(d_wo, 512 * g)))

    add_ada(0)
    if stage >= 1:
        add_ffn("f1")
    if stage >= 2:
        add_ada(1)
        add_mixer(0)
        add_mixer(1)
    if stage >= 3:
        add_ada(2)
        add_ffn("f2")

    class WS:
        def __init__(self):
            self.nxt = 0
            self.cur = 0
            self.free = list(range(NS))
            self.slot_of = {}
            self.reserve = 0

        def pump(self):
            while self.nxt < len(blocks) and len(self.free) > self.reserve:
                s = self.free.pop(0)
                i = self.nxt
                self.nxt += 1
                DMA("pool", RING[:, s], blocks[i][1], "w%d" % s, (), [RTile[s]])
                self.slot_of[i] = s

        def acquire(self, name):
            i = self.cur
            self.cur += 1
            assert blocks[i][0] == name, (blocks[i][0], name)
            if i not in self.slot_of:
                self.pump()
            assert i in self.slot_of, ("weight block not issued", name)
            return i, self.slot_of[i]

        def release(self, i):
            s = self.slot_of.pop(i)
            self.free.append(s)
            self.pump()

        def take_raw(self):
            assert self.free, "no free ring slot for raw use"
            return self.free.pop(0)

        def give_raw(self, s):
            self.free.append(s)
            self.pump()

    W = WS()

    def wcol(s, k, c0):
        return RING[:, s, k // 2, k % 2, c0:c0 + 128]

    scr = Scr()
    WST = scr.take(4096, F32)
    WSM = scr.take(4096, F32)
    MASK = scr.take(512, F32)
    IDENT = scr.take(512, F32)
    ONECOLF = scr.take(32, F32)
    ROWS = scr.take(12288, F32, parts=1)
    ROWSUM = scr.take(4096, F32, parts=1)
    ONEROWF = scr.take(512, F32, parts=1)
    T_WST, T_WSM, T_MASK, T_IDENT, T_ONECOLF, T_ROWS, T_ROWSUM, T_ONEROWF = (Tile() for _ in range(8))
    setup_tiles = [T_WST, T_WSM, T_MASK, T_IDENT, T_ONECOLF, T_ROWS, T_ROWSUM, T_ONEROWF]

    DMA("sp", VECS[:, :], d_vecs, "c", (), [T_VECS])
    DMA("sp", CVEC[:, :], d_cvec, "c", (), [T_CVEC])
    DMA("sp", WST.rearrange("p (h t) -> p h t", h=8), d_wsT, "c", (), [T_WST])
    DMA("sp", MASK, d_mask, "c", (), [T_MASK])
    DMA("sp", IDENT, d_ident, "c", (), [T_IDENT])
    DMA("sp", ROWS, d_rows, "c", (), [T_ROWS])
    for t in (T_VECS, T_CVEC, T_WST, T_MASK, T_IDENT, T_ROWS):
        t.w = {"c": S.dsem["c"]}
    xTv = d_xT.rearrange("(k p) t -> p k t", p=128)
    for tt in range(4):
        DMA("sp", X[:, :, tt * 512:(tt + 1) * 512], xTv[:, :, tt * 512:(tt + 1) * 512], "x%d" % tt, (),
            [XT[k][tt] for k in range(8)])
    W.pump()

    MSET(EPSC[:, :], EPS, [T_EPSC])
    MSET(ONESB[:, :], 1.0 / 1024.0, [T_ONESB])
    MSET(ONEROWB[:, :], 1.0, [T_ONEROWB])
    MSET(ONEROWF, 1.0, [T_ONEROWF])
    MSET(ONECOLF, 1.0, [T_ONECOLF])
    CP(IDB[:, :], IDENT, [T_IDENT], [T_IDB])
    CP(CWB[:, :], VECS[:, C_CW:C_CW + 248], [T_VECS], [T_CWB])
    CP(BVROW[:, :], ROWS[0:1, 0:1024], [T_ROWS], [T_BVROW])
    ACT(CACT[:, :], CVEC[:, :], AF.Silu, [T_CVEC], [T_CACT])
    TT(WSM.rearrange("p (h t) -> p h t", h=8), WST.rearrange("p (h t) -> p h t", h=8),
       MASK.unsqueeze(1).broadcast_to([128, 8, 128]), ALU.mult, [T_WST, T_MASK], [T_WSM])
    CP(WSB[:, :, :], WSM.rearrange("p (h t) -> p h t", h=8), [T_WSM], [T_WSB])
    for g in range(2):
        b = PB.take()
        MM(PS[b][0:1, :], ONECOLF[:, 0:1], WSM[:, g * 512:(g + 1) * 512], True, True,
           [T_ONECOLF, T_WSM], [PT[b]])
        CP(ROWSUM[0:1, g * 512:(g + 1) * 512], PS[b][0:1, :], [PT[b]], [T_ROWSUM])
    CHf = CH[:, :, :].rearrange("p h t -> p (h t)")
    for g in range(2):
        b = PB.take()
        for hq in range(4):
            h = 4 * g + hq
            o_ = PS[b][:, hq * 128:(hq + 1) * 128]
            MM(o_, ROWS[0:1, 2048 + h * 128:2048 + (h + 1) * 128], ROWSUM[0:1, h * 128:(h + 1) * 128],
               True, False, [T_ROWS, T_ROWSUM], [PT[b]])
            MM(o_, ONEROWF[0:1, 0:128], ROWS[0:1, 1024 + h * 128:1024 + (h + 1) * 128],
               False, True, [T_ROWS, T_ONEROWF], [PT[b]])
        CP(CHf[:, g * 512:(g + 1) * 512], PS[b][:, :], [PT[b]], [T_CH])

    def ada_phase(g):
        b = PB.take(hold=True)
        for blk in range(6):
            i, s = W.acquire("ada%d_%d" % (g, blk))
            for cc in range(4):
                jl = blk * 4 + cc
                for k in range(8):
                    MM(PS[b][:, jl:jl + 1], wcol(s, k, cc * 128), CACT[:, k:k + 1], k == 0, k == 7,
                       [RTile[s], T_CACT], [PT[b]])
            W.release(i)
        TT(MOD[:, 24 * g:24 * g + 24], PS[b][:, 0:24], VECS[:, C_ADAB + 24 * g:C_ADAB + 24 * g + 24], ALU.add,
           [PT[b], T_VECS], [MODT[g]])
        PB.drop(b)
        ncol = (C_NF1, C_NMIX, C_NF2)[g]
        acol = (0, 16, 24)[g]
        STT(DER[:, acol:acol + 8], MOD[:, 24 * g + 8:24 * g + 16], 1.0, VECS[:, ncol:ncol + 8], ALU.add, ALU.mult,
            [MODT[g], T_VECS], [DERT[g]])
        if g != 1:
            gcol = (8, None, 32)[g]
            TS(DER[:, gcol:gcol + 8], MOD[:, 24 * g + 16:24 * g + 24], 0.5, None, ALU.mult, None,
               [MODT[g], DERT[g]], [DERT[g]])

    def stats_tile(tt):
        b = PB.take()
        for k in range(8):
            sq, sqt = SQ.next()
            ACT(sq, Xap(k, tt), AF.Square, [XT[k][tt]], [sqt])
            MM(PS[b][:, :], ONESB[:, :], sq, k == 0, k == 7, [sqt, T_ONESB], [PT[b]])
        r, rt = RT.next()
        RSQRT(r, PS[b][:, :], [PT[b], T_EPSC], rt)
        return r, rt

    def norm_phase(tts, g, hmap):
        acol = (0, 16, 24)[g]
        bcol = 24 * g
        for tt in tts:
            r, rt = stats_tile(tt)
            for k in range(8):
                tm, tmt = TMPN.next()
                TT(tm, Xap(k, tt), r, ALU.mult, [XT[k][tt], rt], [tmt])
                hap, htile = hmap(k, tt)
                ACT(hap, tm, AF.Identity, [tmt, DERT[g], MODT[g]], [htile],
                    bias=MOD[:, bcol + k:bcol + k + 1], scale=DER[:, acol + k:acol + k + 1])

    def ffn_phase(f, g):
        gcol = (8, None, 32)[g]
        sc = Scr()
        ATb = [sc.take(16384, BF16).rearrange("p (s t) -> p s t", s=4) for _ in range(2)]
        SGb = Buf([sc.take(2048, F32) for _ in range(2)])
        ATT = [[[Tile() for _ in range(4)] for _ in range(4)] for _ in range(2)]
        new_tiles = [t for a in ATT for b_ in a for t in b_] + SGb.tiles
        alias_into(new_tiles, ffn_phase.prev_tiles)
        ffn_phase.prev_tiles = new_tiles

        def GU(j):
            ig, sg_ = W.acquire("%sg%d" % (f, j))
            iu, su = W.acquire("%su%d" % (f, j))
            buf = j % 2
            for hs in range(4):
                for tt in range(4):
                    bg = PB.take()
                    bu = PB.take()
                    for k in range(8):
                        MM(PS[bg][:, :], wcol(sg_, k, hs * 128), Hap(k, tt), k == 0, k == 7,
                           [RTile[sg_], HT[k][tt]], [PT[bg]])
                    for k in range(8):
                        MM(PS[bu][:, :], wcol(su, k, hs * 128), Hap(k, tt), k == 0, k == 7,
                           [RTile[su], HT[k][tt]], [PT[bu]])
                    sgap, sgt = SGb.next()
                    ACT(sgap, PS[bg][:, :], AF.Silu, [PT[bg]], [sgt])
                    TT(ATb[buf][:, hs, tt * 512:(tt + 1) * 512], PS[bu][:, :], sgap, ALU.mult,
                       [PT[bu], sgt], [ATT[buf][hs][tt]])
            W.release(ig)
            W.release(iu)

        def DN(j):
            idd, sd = W.acquire("%sd%d" % (f, j))
            buf = j % 2
            for tt in range(4):
                for o in range(8):
                    b = PB.take()
                    for s in range(4):
                        MM(PS[b][:, :], RING[:, sd, s, o // 4, (o % 4) * 128:(o % 4 + 1) * 128],
                           ATb[buf][:, s, tt * 512:(tt + 1) * 512], s == 0, s == 3,
                           [RTile[sd], ATT[buf][s][tt]], [PT[b]])
                    STT(Xap(o, tt), PS[b][:, :], DER[:, gcol + o:gcol + o + 1], Xap(o, tt), ALU.mult, ALU.add,
                        [PT[b], XT[o][tt], DERT[g]], [XT[o][tt]])
            W.release(idd)

        GU(0)
        for j in range(1, 8):
            GU(j)
            DN(j - 1)
        DN(7)

    ffn_phase.prev_tiles = setup_tiles

    mix_state = {"prev": None}

    def mixer_half(hh, first):
        sc = Scr()
        UZ = sc.take(8 * 1056 * 2, BF16).rearrange("p (o t) -> p o t", o=8)
        SGb = Buf([sc.take(2048, F32) for _ in range(2)])
        wbase = sc.off
        VGb = Buf([sc.take(4096, F32) for _ in range(2)])
        VNb = Buf([sc.take(2048, BF16) for _ in range(2)])
        TMPV = sc.take(4096, F32).rearrange("p (h t) -> p h t", h=8)
        T_TMPV = Tile()
        sc.off = wbase
        C32 = sc.take(16384, F32).rearrange("p (o t) -> p o t", o=8)
        C32T = [Tile() for _ in range(8)]
        m2_tiles = VGb.tiles + VNb.tiles + [T_TMPV]
        m5_tiles = C32T
        if first:
            UZT = [[Tile() for _ in range(2)] for _ in range(8)]
            ZHT = Tile()
            alias_into([t for a in UZT for t in a] + [ZHT] + SGb.tiles + m2_tiles, ffn_phase.prev_tiles)
            mix_state["UZT"] = UZT
            mix_state["ZHT"] = ZHT
            mix_state["SGt"] = SGb.tiles
            MSET(UZ[:, :, 0:32], 0.0, [ZHT])
        else:
            UZT = mix_state["UZT"]
            ZHT = mix_state["ZHT"]
            SGb.tiles = mix_state["SGt"]
            alias_into(m2_tiles, mix_state["m5"])
        mix_state["m5"] = m5_tiles

        def Uap(o, tl):
            return UZ[:, o, 32 + tl * 512:32 + (tl + 1) * 512]

        def bin_col(part, o):
            c = C_BIN + part * 8 + o
            return VECS[:, c:c + 1]

        tts = [2 * hh, 2 * hh + 1]
        norm_phase(tts, 1, lambda k, tt: (Hap(k, tt - 2 * hh), HT[k][tt - 2 * hh]))

        for blk in range(2):
            i, s = W.acquire("m%din%d" % (hh, blk))
            for oc in range(4):
                o = blk * 4 + oc
                for tl in range(2):
                    b = PB.take()
                    for k in range(8):
                        MM(PS[b][:, :], wcol(s, k, oc * 128), Hap(k, tl), k == 0, k == 7,
                           [RTile[s], HT[k][tl]], [PT[b]])
                    ACT(Uap(o, tl), PS[b][:, :], AF.Gelu_apprx_tanh, [PT[b], T_VECS], [UZT[o][tl]],
                        bias=bin_col(0, o))
            W.release(i)

        i2, s2 = W.acquire("m%din2" % hh)
        i3, s3 = W.acquire("m%din3" % hh)
        for tc in range(8):
            tl = tc // 4
            c0 = tl * 512 + (tc % 4) * 128
            vg, vgt = VGb.next()
            st_ap, st_t = STATSB.next()
            mv, mvt = MVTB.next()
            for nh in range(2):
                s = (s2, s3)[nh]
                b = PB.take()
                for k in range(8):
                    MM(PS[b][:, :], H[:, k, c0:c0 + 128], RING[:, s, k // 2, k % 2, :], k == 0, False,
                       [HT[k][tl], RTile[s]], [PT[b]])
                MM(PS[b][:, :], ONEROWB[0:1, :], BVROW[0:1, nh * 512:(nh + 1) * 512], False, True,
                   [T_ONEROWB, T_BVROW], [PT[b]])
                ACT(vg[:, nh * 512:(nh + 1) * 512], PS[b][:, :], AF.Gelu_apprx_tanh, [PT[b]], [vgt])
                S.emit("dve", lambda e, o=st_ap[:, nh * 6:(nh + 1) * 6], i_=vg[:, nh * 512:(nh + 1) * 512]:
                       e.bn_stats(out=o, in_=i_), [vgt], [st_t])
            S.emit("dve", lambda e, o=mv[:, 0:2], i_=st_ap: e.bn_aggr(out=o, in_=i_), [st_t], [mvt])
            RSQRT(mv[:, 2:3], mv[:, 1:2], [mvt, T_EPSC], mvt)
            vn, vnt = VNb.next()
            TS(vn, vg, mv[:, 0:1], mv[:, 2:3], ALU.subtract, ALU.mult, [vgt, mvt], [vnt])
            b2 = [PB.take(), PB.take()]
            for h in range(8):
                MM(PS[b2[h // 4]][:, (h % 4) * 128:(h % 4 + 1) * 128], vn[:, h * 128:(h + 1) * 128], WSB[:, h, :],
                   True, True, [vnt, T_WSB], [PT[b2[h // 4]]])
            for h in range(8):
                STT(TMPV[:, h, :], PS[b2[h // 4]][:, (h % 4) * 128:(h % 4 + 1) * 128],
                    VECS[:, C_SLG + h:C_SLG + h + 1], CH[:, h, :], ALU.mult, ALU.add,
                    [PT[b2[h // 4]], T_VECS, T_CH], [T_TMPV])
            ucols = UZ[:, :, 32 + c0:32 + c0 + 128]
            ut = [UZT[o][tl] for o in range(8)]
            TT(ucols, TMPV, ucols, ALU.mult, [T_TMPV] + ut, ut)
        W.release(i2)
        W.release(i3)

        for g in range(2):
            ig, sgx = W.acquire("m%din%d" % (hh, 8 + g))
            ia, sa = W.acquire("m%dwa%d" % (hh, g))
            for oc in range(4):
                o = g * 4 + oc
                for tl in range(2):
                    bA = PB.take()
                    bB = PB.take()
                    for k in range(8):
                        MM(PS[bA][:, :], wcol(sgx, k, oc * 128), Hap(k, tl), k == 0, k == 7,
                           [RTile[sgx], HT[k][tl]], [PT[bA]])
                    for k in range(8):
                        MM(PS[bB][:, :], wcol(sa, k, oc * 128), Uap(k, tl), k == 0, k == 7,
                           [RTile[sa], UZT[k][tl]], [PT[bB]])
                    sgap, sgt = SGb.next()
                    ACT(sgap, PS[bA][:, :], AF.Sigmoid, [PT[bA], T_VECS], [sgt], bias=bin_col(4, o))
                    TT(Hap(o, 2 + tl), PS[bB][:, :], sgap, ALU.mult, [PT[bB], sgt], [HT[o][2 + tl]])
            W.release(ig)
            W.release(ia)

        W.reserve = 2
        for g in range(2):
            iv, sv = W.acquire("m%din%d" % (hh, 4 + g))
            ic, scx = W.acquire("m%din%d" % (hh, 6 + g))
            for oc in range(4):
                o = g * 4 + oc
                for tl in range(2):
                    bA = PB.take()
                    bB = PB.take()
                    for k in range(8):
                        MM(PS[bA][:, :], wcol(sv, k, oc * 128), Hap(k, tl), k == 0, k == 7,
                           [RTile[sv], HT[k][tl]], [PT[bA]])
                    for k in range(8):
                        MM(PS[bB][:, :], wcol(scx, k, oc * 128), Hap(k, tl), k == 0, k == 7,
                           [RTile[scx], HT[k][tl]], [PT[bB]])
                    sgap, sgt = SGb.next()
                    ACT(sgap, PS[bB][:, :], AF.Sigmoid, [PT[bB], T_VECS], [sgt], bias=bin_col(3, o))
                    STT(Uap(o, tl), PS[bA][:, :], bin_col(2, o), sgap, ALU.add, ALU.mult,
                        [PT[bA], sgt, T_VECS], [UZT[o][tl]])
            W.release(iv)
            W.release(ic)

        alias_into(m5_tiles, m2_tiles)
        if hh == 0:
            CP(HALO[:, :, 2:32], UZ[:, :, 32 + 994:32 + 1024], [UZT[o][1] for o in range(8)], [T_HALO])
        else:
            CP(UZ[:, :, 2:32], HALO[:, :, 2:32], [T_HALO], [ZHT])
        for st in (1, 0):
            bm = PB.take(hold=True)
            bq = PB.take(hold=True)
            for o in range(8):
                sl = W.take_raw()
                DG = RING[:, sl].rearrange("p a b n -> p (a b n)")[:, 0:3968].rearrange("p (k j) -> p k j", k=31)
                TT(DG, IDB[:, :].unsqueeze(1).broadcast_to([128, 31, 128]),
                   CWB[:, o * 31:(o + 1) * 31].unsqueeze(2).broadcast_to([128, 31, 128]), ALU.mult,
                   [T_IDB, T_CWB], [RTile[sl]])
                b = PB.take()
                rds = [RTile[sl], UZT[o][st], (UZT[o][st - 1] if st > 0 else ZHT)]
                for k in range(31):
                    cc = 2 + st * 512 + k
                    MM(PS[b][:, :], DG[:, k, :], UZ[:, o, cc:cc + 512], k == 0, k == 30, rds, [PT[b]])
                W.give_raw(sl)
                cvb = VECS[:, C_CVB + o:C_CVB + o + 1]
                ACT(C32[:, o, :], PS[b][:, :], AF.Identity, [PT[b], T_VECS], [C32T[o]], bias=cvb)
                cb, cbt = SQ.next()
                cq, cqt = SQ.next()
                ACT(cb, PS[b][:, :], AF.Identity, [PT[b], T_VECS], [cbt], bias=cvb)
                ACT(cq, PS[b][:, :], AF.Square, [PT[b], T_VECS], [cqt], bias=cvb)
                MM(PS[bm][:, :], ONESB[:, :], cb, o == 0, o == 7, [cbt, T_ONESB], [PT[bm]])
                MM(PS[bq][:, :], ONESB[:, :], cq, o == 0, o == 7, [cqt, T_ONESB], [PT[bq]])
            mean_sb, mean_t = RT.next()
            rs, rs_t = RT.next()
            tq, tq_t = TMPN.next()
            mr, mr_t = TMPN.next()
            CP(mean_sb, PS[bm][:, :], [PT[bm]], [mean_t])
            TT(tq, mean_sb, mean_sb, ALU.mult, [mean_t], [tq_t])
            TT(tq, PS[bq][:, :], tq, ALU.subtract, [PT[bq], tq_t], [tq_t])
            RSQRT(rs, tq, [tq_t, T_EPSC], rs_t)
            TT(mr, mean_sb, rs, ALU.mult, [mean_t, rs_t], [mr_t])
            PB.drop(bm)
            PB.drop(bq)
            for o in range(8):
                TT(C32[:, o, :], C32[:, o, :], rs, ALU.mult, [C32T[o], rs_t], [C32T[o]])
                TT(C32[:, o, :], C32[:, o, :], mr, ALU.subtract, [C32T[o], mr_t], [C32T[o]])
                ACT(Uap(o, st), C32[:, o, :], AF.Silu, [C32T[o], T_VECS], [UZT[o][st]],
                    bias=VECS[:, C_CLB + o:C_CLB + o + 1], scale=VECS[:, C_CLG + o:C_CLG + o + 1])
        W.reserve = 0
        W.pump()

        for g in range(2):
            ig, sgx = W.acquire("m%din%d" % (hh, 10 + g))
            ib, sbx = W.acquire("m%dwb%d" % (hh, g))
            for oc in range(4):
                o = g * 4 + oc
                for tl in range(2):
                    bA = PB.take()
                    bB = PB.take()
                    for k in range(8):
                        MM(PS[bA][:, :], wcol(sgx, k, oc * 128), Hap(k, tl), k == 0, k == 7,
                           [RTile[sgx], HT[k][tl]], [PT[bA]])
                    for k in range(8):
                        MM(PS[bB][:, :], wcol(sbx, k, oc * 128), Uap(k, tl), k == 0, k == 7,
                           [RTile[sbx], UZT[k][tl]], [PT[bB]])
                    sgap, sgt = SGb.next()
                    ACT(sgap, PS[bA][:, :], AF.Sigmoid, [PT[bA], T_VECS], [sgt], bias=bin_col(5, o))
                    tm, tmt = TMPN.next()
                    TT(tm, PS[bB][:, :], sgap, ALU.mult, [PT[bB], sgt], [tmt])
                    TT(Hap(o, 2 + tl), Hap(o, 2 + tl), tm, ALU.add, [HT[o][2 + tl], tmt], [HT[o][2 + tl]])
            W.release(ig)
            W.release(ib)

        for g in range(2):
            io, so = W.acquire("m%dwo%d" % (hh, g))
            for oc in range(4):
                o = g * 4 + oc
                for tl in range(2):
                    b = PB.take()
                    for k in range(8):
                        MM(PS[b][:, :], wcol(so, k, oc * 128), Hap(k, 2 + tl), k == 0, k == 7,
                           [RTile[so], HT[k][2 + tl]], [PT[b]])
                    tt = 2 * hh + tl
                    STT(Xap(o, tt), PS[b][:, :], MOD[:, 40 + o:41 + o], Xap(o, tt), ALU.mult, ALU.add,
                        [PT[b], XT[o][tt], MODT[1]], [XT[o][tt]])
            W.release(io)
        return [t for a in UZT for t in a] + [ZHT] + SGb.tiles + m2_tiles + m5_tiles

    ada_phase(0)
    if stage >= 1:
        norm_phase(range(4), 0, lambda k, tt: (Hap(k, tt), HT[k][tt]))
        ffn_phase("f1", 0)
    if stage >= 2:
        ada_phase(1)
        mixer_half(0, True)
        mtiles = mixer_half(1, False)
        ffn_phase.prev_tiles = mtiles
    if stage >= 3:
        ada_phase(2)
        norm_phase(range(4), 2, lambda k, tt: (Hap(k, tt), HT[k][tt]))
        ffn_phase("f2", 2)
    outv = d_out.rearrange("(k p) t -> p k t", p=128)
    T_OUT = Tile()
    for tt in range(4):
        if stage >= 4:
            r, rt = stats_tile(tt)
            for k in range(8):
                STT(Xap(k, tt), Xap(k, tt), VECS[:, C_NFIN + k:C_NFIN + k + 1], r, ALU.mult, ALU.mult,
                    [XT[k][tt], rt, T_VECS], [XT[k][tt]])
        DMA("sp", outv[:, :, tt * 512:(tt + 1) * 512], X[:, :, tt * 512:(tt + 1) * 512], "out",
            [XT[k][tt] for k in range(8)], [T_OUT])
    T_OUT.w = {"out": S.dsem["out"]}
    S.emit("sp", None, [T_OUT], ())

    assert W.cur == len(blocks) and W.nxt == len(blocks), (W.cur, W.nxt, len(blocks))

    sig = {e: set() for e in ("pe", "act", "dve", "pool")}
    for e in S.ENG:
        for fn, waits, tok in S.q[e]:
            for k, v in waits:
                if k in sig:
                    sig[k].add(v)
    rank = {}
    for e, st_ in sig.items():
        rank[e] = {v: i + 1 for i, v in enumerate(sorted(st_))}

    def replay(ename, handle):
        for fn, waits, tok in S.q[ename]:
            for k, v in waits:
                if k in sem_eng:
                    handle.wait_ge(sem_eng[k], rank[k][v])
                else:
                    handle.wait_ge(sem_dma[k], v)
            if fn is None:
                continue
            inst = fn(handle)
            if tok[0] in sem_dma:
                inst.then_inc(sem_dma[tok[0]], 16)
            elif tok[1] in sig[tok[0]]:
                inst.then_inc(sem_eng[tok[0]], 1)

    with nc.Block() as block:
        @block.sync
        def _(e):
            replay("sp", e)

        @block.gpsimd
        def _(e):
            replay("pool", e)

        @block.scalar
        def _(e):
            replay("act", e)

        @block.vector
        def _(e):
            replay("dve", e)

        @block.tensor
        def _(e):
            replay("pe", e)

    es.close()
    return nc


def prep_inputs(inp):
    g = lambda n: np.asarray(inp[n], dtype=np.float32)
    x = g("x")
    c = g("c")
    vecs = np.zeros((128, NV), np.float32)

    def put(col, v):
        v = np.asarray(v, np.float32).reshape(-1, 128).T
        vecs[:, col:col + v.shape[1]] = v

    put(C_NF1, g("norm_ffn1")[0])
    put(C_NMIX, g("norm_mix")[0])
    put(C_NF2, g("norm_ffn2")[0])
    put(C_NFIN, g("norm_final"))
    put(C_BIN, g("mix_b_in")[0])
    put(C_SLG, g("sgu_ln_g")[0])
    put(C_SLB, g("sgu_ln_b")[0])
    put(C_CVB, g("conv_b")[0])
    put(C_CLG, g("conv_ln_g")[0])
    put(C_CLB, g("conv_ln_b")[0])
    put(C_ADAB, g("ada_b")[0])
    cw = g("conv_w")[0]
    for o in range(8):
        vecs[:, C_CW + o * 31:C_CW + (o + 1) * 31] = cw[:, o * 128:(o + 1) * 128].T
    wsT = np.ascontiguousarray(g("sgu_w_s")[0].transpose(2, 0, 1))
    s_idx = np.arange(128)
    mask = (s_idx[:, None] <= s_idx[None, :]).astype(np.float32)
    ident = np.eye(128, dtype=np.float32)
    rows = np.concatenate([g("mix_b_in")[0][1024:2048], g("sgu_b_s")[0].reshape(-1), g("sgu_ln_b")[0]])[None, :]
    rows = np.ascontiguousarray(rows, dtype=np.float32)
    shared = {
        "vecs": vecs, "wsT": wsT, "mask": mask, "ident": ident, "rows": rows,
        "ada_w": np.ascontiguousarray(g("ada_w")[0]),
        "f1g": np.ascontiguousarray(g("ffn1_w_gate")[0]), "f1u": np.ascontiguousarray(g("ffn1_w_up")[0]),
        "f1d": np.ascontiguousarray(g("ffn1_w_down")[0]),
        "f2g": np.ascontiguousarray(g("ffn2_w_gate")[0]), "f2u": np.ascontiguousarray(g("ffn2_w_up")[0]),
        "f2d": np.ascontiguousarray(g("ffn2_w_down")[0]),
        "w_in": np.ascontiguousarray(g("mix_w_in")[0]),
        "w_a": np.ascontiguousarray(g("w_branch_a")[0]), "w_b": np.ascontiguousarray(g("w_branch_b")[0]),
        "w_o": np.ascontiguousarray(g("w_out")[0]),
    }
    in_maps = []
    for b in range(8):
        m = dict(shared)
        m["xT"] = np.ascontiguousarray(x[b].T)
        m["cvec"] = np.ascontiguousarray(c[b].reshape(8, 128).T)
        in_maps.append(m)
    return in_maps


def kernel(**inputs):
    in_maps = prep_inputs(inputs)
    nc = build_program(STAGE if STAGE < 9 else 4)
    res = run_bass_kernel_spmd(nc, in_maps, core_ids=list(range(8)))
    out = np.stack([np.asarray(r["outT"], dtype=np.float32).T for r in res.results], axis=0)
    return np.ascontiguousarray(out)
```

```python
import os
from contextlib import ExitStack

import numpy as np
import concourse.bass as bass
import concourse.mybir as mybir
from concourse.bass_utils import run_bass_kernel_spmd

F32 = mybir.dt.float32
BF16 = mybir.dt.bfloat16
AF = mybir.ActivationFunctionType
ALU = mybir.AluOpType

EPS = 1e-6
NS = 6
NV = 440
STAGE = int(os.environ.get("KSTAGE", "9"))

C_NF1, C_NMIX, C_NF2, C_NFIN = 0, 8, 16, 24
C_BIN = 32
C_SLG, C_SLB = 80, 88
C_CVB, C_CLG, C_CLB = 96, 104, 112
C_ADAB = 120
C_CW = 192


class Tile:
    __slots__ = ("w", "r")

    def __init__(self):
        self.w = {}
        self.r = {}


def alias_into(new_tiles, old_tiles):
    w, r = {}, {}
    for t in old_tiles:
        for k, v in t.w.items():
            if w.get(k, 0) < v:
                w[k] = v
        for k, v in t.r.items():
            if r.get(k, 0) < v:
                r[k] = v
    for t in new_tiles:
        t.w = dict(w)
        t.r = dict(r)


class Sched:
    ENG = ("pe", "act", "dve", "pool", "sp")

    def __init__(self):
        self.q = {e: [] for e in self.ENG}
        self.seq = {e: 0 for e in self.ENG}
        self.known = {e: {} for e in self.ENG}
        self.dsem = {}

    def emit(self, eng, fn, reads=(), writes=(), dma=None):
        raw, oth = {}, {}
        for t in reads:
            for k, v in t.w.items():
                if raw.get(k, 0) < v:
                    raw[k] = v
        for t in writes:
            for k, v in t.w.items():
                if oth.get(k, 0) < v:
                    oth[k] = v
            for k, v in t.r.items():
                if oth.get(k, 0) < v:
                    oth[k] = v
        waits = []
        kn = self.known[eng]
        for k, v in raw.items():
            if k == eng and eng == "pe":
                continue
            if kn.get(k, 0) >= v:
                continue
            kn[k] = v
            waits.append((k, v))
        for k, v in oth.items():
            if k == eng:
                continue
            if kn.get(k, 0) >= v:
                continue
            kn[k] = v
            waits.append((k, v))
        if dma is None:
            self.seq[eng] += 1
            tok = (eng, self.seq[eng])
        else:
            self.dsem[dma] = self.dsem.get(dma, 0) + 16
            tok = (dma, self.dsem[dma])
        self.q[eng].append((fn, waits, tok))
        for t in reads:
            if t.r.get(tok[0], 0) < tok[1]:
                t.r[tok[0]] = tok[1]
        for t in writes:
            t.w = {tok[0]: tok[1]}
            t.r = {}
        return tok


class Banks:
    def __init__(self):
        self.next = 0
        self.held = set()

    def take(self, hold=False):
        for _ in range(16):
            b = self.next
            self.next = (self.next + 1) % 8
            if b not in self.held:
                if hold:
                    self.held.add(b)
                return b
        raise RuntimeError("no psum bank")

    def drop(self, b):
        self.held.discard(b)


class Buf:
    def __init__(self, aps):
        self.aps = aps
        self.tiles = [Tile() for _ in aps]
        self.i = 0

    def next(self):
        j = self.i % len(self.aps)
        self.i += 1
        return self.aps[j], self.tiles[j]


def build_program(stage=STAGE):
    nc = bass.Bass("TRN2", target_bir_lowering=False)
    S = Sched()
    PB = Banks()
    es = ExitStack()

    def din(name, shape):
        return nc.dram_tensor(name, list(shape), F32, kind="ExternalInput").ap()

    d_xT = din("xT", [1024, 2048])
    d_cvec = din("cvec", [128, 8])
    d_vecs = din("vecs", [128, NV])
    d_wsT = din("wsT", [128, 8, 128])
    d_mask = din("mask", [128, 128])
    d_ident = din("ident", [128, 128])
    d_rows = din("rows", [1, 3072])
    d_ada = din("ada_w", [1024, 9216])
    d_f = {}
    for f in ("f1", "f2"):
        d_f[f + "g"] = din(f + "g", [1024, 4096])
        d_f[f + "u"] = din(f + "u", [1024, 4096])
        d_f[f + "d"] = din(f + "d", [4096, 1024])
    d_win = din("w_in", [1024, 6144])
    d_wa = din("w_a", [1024, 1024])
    d_wb = din("w_b", [1024, 1024])
    d_wo = din("w_o", [1024, 1024])
    d_out = nc.dram_tensor("outT", [1024, 2048], F32, kind="ExternalOutput").ap()

    def sb(name, shape, dt):
        return es.enter_context(nc.sbuf_tensor(name, list(shape), dt))

    X = sb("X", [128, 8, 2048], F32)
    H = sb("H", [128, 8, 2048], BF16)
    RING = sb("RING", [128, NS, 4, 2, 512], BF16)
    VECS = sb("VECS", [128, NV], F32)
    MOD = sb("MOD", [128, 72], F32)
    DER = sb("DER", [128, 40], F32)
    CVEC = sb("CVEC", [128, 8], F32)
    CACT = sb("CACT", [128, 8], BF16)
    ONESB = sb("ONESB", [128, 128], BF16)
    IDB = sb("IDB", [128, 128], BF16)
    WSB = sb("WSB", [128, 8, 128], BF16)
    CH = sb("CH", [128, 8, 128], F32)
    BVROW = sb("BVROW", [1, 1024], BF16)
    ONEROWB = sb("ONEROWB", [1, 128], BF16)
    CWB = sb("CWB", [128, 248], BF16)
    HALO = sb("HALO", [128, 8, 32], BF16)
    RTb = sb("RTb", [128, 2, 512], F32)
    SQb = sb("SQb", [128, 4, 512], BF16)
    TMPNb = sb("TMPNb", [128, 2, 512], F32)
    STATS = sb("STATS", [128, 2, 12], F32)
    MVT = sb("MVT", [128, 2, 4], F32)
    EPSC = sb("EPSC", [128, 8], F32)
    SCRN = 18688
    SCR = sb("SCR", [128, SCRN], BF16)

    PS = [es.enter_context(nc.psum_tensor("ps%d" % i, [128, 512], F32)) for i in range(8)]
    PT = [Tile() for _ in range(8)]

    sem_eng = {e: es.enter_context(nc.semaphore("s_" + e)) for e in ("pe", "act", "dve", "pool")}
    dma_keys = ["c", "out"] + ["x%d" % i for i in range(4)] + ["w%d" % i for i in range(NS)]
    sem_dma = {k: es.enter_context(nc.semaphore("d_" + k)) for k in dma_keys}

    def MM(out, lhsT, rhs, start, stop, reads, writes):
        S.emit("pe", lambda e, o=out, l=lhsT, r=rhs, a=start, b=stop:
               e.matmul(o, lhsT=l, rhs=r, start=a, stop=b), reads, writes)

    def ACT(out, in_, func, reads, writes, bias=None, scale=None):
        kw = {}
        if bias is not None:
            kw["bias"] = bias
        if scale is not None:
            kw["scale"] = scale
        S.emit("act", lambda e, o=out, i=in_, f=func, kw=kw:
               e.activation(out=o, in_=i, func=f, **kw), reads, writes)

    def TT(out, in0, in1, op, reads, writes, eng="dve"):
        S.emit(eng, lambda e, o=out, a=in0, b=in1, p=op:
               e.tensor_tensor(out=o, in0=a, in1=b, op=p), reads, writes)

    def TS(out, in0, s1, s2, op0, op1, reads, writes, eng="dve"):
        if op1 is None:
            S.emit(eng, lambda e, o=out, a=in0, x=s1, p=op0:
                   e.tensor_scalar(out=o, in0=a, scalar1=x, scalar2=None, op0=p), reads, writes)
        else:
            S.emit(eng, lambda e, o=out, a=in0, x=s1, y=s2, p=op0, q=op1:
                   e.tensor_scalar(out=o, in0=a, scalar1=x, scalar2=y, op0=p, op1=q), reads, writes)

    def STT(out, in0, scalar, in1, op0, op1, reads, writes):
        S.emit("dve", lambda e, o=out, a=in0, s=scalar, b=in1, p=op0, q=op1:
               e.scalar_tensor_tensor(out=o, in0=a, scalar=s, in1=b, op0=p, op1=q), reads, writes)

    def RSQRT(out, in_, reads, wtile):
        ACT(out, in_, AF.Sqrt, reads, [wtile], bias=EPSC[:in_.shape[0], 0:1])
        S.emit("dve", lambda e, o=out: e.reciprocal(out=o, in_=o), [wtile], [wtile])

    def CP(out, in_, reads, writes, eng="dve"):
        S.emit(eng, lambda e, o=out, i=in_: e.tensor_copy(out=o, in_=i), reads, writes)

    def MSET(ap, val, writes, eng="dve"):
        S.emit(eng, lambda e, a=ap, v=val: e.memset(a, v), (), writes)

    def DMA(q, out, in_, key, reads, writes):
        S.emit(q, lambda e, o=out, i=in_: e.dma_start(out=o, in_=i), reads, writes, dma=key)

    XT = [[Tile() for _ in range(4)] for _ in range(8)]
    HT = [[Tile() for _ in range(4)] for _ in range(8)]
    RTile = [Tile() for _ in range(NS)]
    T_VECS, T_CVEC, T_CACT, T_ONESB, T_IDB, T_WSB, T_CH = (Tile() for _ in range(7))
    T_BVROW, T_ONEROWB, T_CWB, T_HALO, T_EPSC = (Tile() for _ in range(5))
    MODT = [Tile() for _ in range(3)]
    DERT = [Tile() for _ in range(3)]
    RT = Buf([RTb[:, i, :] for i in range(2)])
    SQ = Buf([SQb[:, i, :] for i in range(4)])
    TMPN = Buf([TMPNb[:, i, :] for i in range(2)])
    STATSB = Buf([STATS[:, i, :] for i in range(2)])
    MVTB = Buf([MVT[:, i, :] for i in range(2)])

    def Xap(k, tt):
        return X[:, k, tt * 512:(tt + 1) * 512]

    def Hap(k, tt):
        return H[:, k, tt * 512:(tt + 1) * 512]

    class Scr:
        def __init__(self):
            self.off = 0

        def take(self, nbytes, dt, parts=None):
            assert self.off % 32 == 0
            a = self.off // 2
            n2 = nbytes // 2
            assert a + n2 <= SCRN, (a, n2)
            v = SCR[:, a:a + n2] if parts is None else SCR[0:parts, a:a + n2]
            self.off += (nbytes + 31) // 32 * 32
            if dt is F32:
                v = v.bitcast(F32)
            return v

    def colblock(Wd_, c0):
        return Wd_.rearrange("(a b p) n -> p a b n", a=4, b=2, p=128)[:, :, :, c0:c0 + 512]

    def rowblock(Wd_, j):
        return Wd_.rearrange("(s p) (c n) -> p s c n", p=128, c=2)[:, 4 * j:4 * j + 4]

    blocks = []

    def add_ada(g):
        for blk in range(6):
            blocks.append(("ada%d_%d" % (g, blk), colblock(d_ada, 3072 * g + 512 * blk)))

    def add_ffn(f, before_last=None):
        def gu(j):
            blocks.append(("%sg%d" % (f, j), colblock(d_f[f + "g"], 512 * j)))
            blocks.append(("%su%d" % (f, j), colblock(d_f[f + "u"], 512 * j)))

        def dn(j):
            blocks.append(("%sd%d" % (f, j), rowblock(d_f[f + "d"], j)))
        gu(0)
        for j in range(1, 8):
            gu(j)
            dn(j - 1)
        if before_last is not None:
            before_last()
        dn(7)

    def add_mixer(hh, before_wo=None):
        def inb(i):
            blocks.append(("m%din%d" % (hh, i), colblock(d_win, 512 * i)))
        for i in (0, 1, 2, 3):
            inb(i)
        for g in range(2):
            inb(8 + g)
            blocks.append(("m%dwa%d" % (hh, g), colblock(d_wa, 512 * g)))
        for g in range(2):
            inb(4 + g)
            inb(6 + g)
        for g in range(2):
            inb(10 + g)
            blocks.append(("m%dwb%d" % (hh, g), colblock(d_wb, 512 * g)))
        if before_wo is not None:
            before_wo()
        for g in range(2):
            blocks.append(("m%dwo%d" % (hh, g), colblock(d_wo, 512 * g)))

    add_ada(0)
    add_ffn("f1", before_last=lambda: add_ada(1))
    add_mixer(0)
    add_mixer(1, before_wo=lambda: add_ada(2))
    add_ffn("f2")

    class WS:
        def __init__(self):
            self.nxt = 0
            self.cur = 0
            self.free = list(range(NS))
            self.slot_of = {}
            self.reserve = 0

        def pump(self):
            while self.nxt < len(blocks) and len(self.free) > self.reserve:
                s = self.free.pop(0)
                i = self.nxt
                self.nxt += 1
                DMA("pool", RING[:, s], blocks[i][1], "w%d" % s, (), [RTile[s]])
                self.slot_of[i] = s

        def acquire(self, name):
            i = self.cur
            self.cur += 1
            assert blocks[i][0] == name, (blocks[i][0], name)
            if i not in self.slot_of:
                self.pump()
            assert i in self.slot_of, ("weight block not issued", name)
            return i, self.slot_of[i]

        def release(self, i):
            s = self.slot_of.pop(i)
            self.free.append(s)
            self.pump()

        def take_raw(self):
            assert self.free, "no free ring slot for raw use"
            return self.free.pop(0)

        def give_raw(self, s):
            self.free.append(s)
            self.pump()

    W = WS()

    def wcol(s, k, c0):
        return RING[:, s, k // 2, k % 2, c0:c0 + 128]

    scr = Scr()
    WST = scr.take(4096, F32)
    WSM = scr.take(4096, F32)
    MASK = scr.take(512, F32)
    IDENT = scr.take(512, F32)
    ONECOLF = scr.take(32, F32)
    ROWS = scr.take(12288, F32, parts=1)
    ROWSUM = scr.take(4096, F32, parts=1)
    ONEROWF = scr.take(512, F32, parts=1)
    T_WST, T_WSM, T_MASK, T_IDENT, T_ONECOLF, T_ROWS, T_ROWSUM, T_ONEROWF = (Tile() for _ in range(8))
    setup_tiles = [T_WST, T_WSM, T_MASK, T_IDENT, T_ONECOLF, T_ROWS, T_ROWSUM, T_ONEROWF]

    DMA("sp", VECS[:, :], d_vecs, "c", (), [T_VECS])
    DMA("sp", CVEC[:, :], d_cvec, "c", (), [T_CVEC])
    DMA("sp", WST.rearrange("p (h t) -> p h t", h=8), d_wsT, "c", (), [T_WST])
    DMA("sp", MASK, d_mask, "c", (), [T_MASK])
    DMA("sp", IDENT, d_ident, "c", (), [T_IDENT])
    DMA("sp", ROWS, d_rows, "c", (), [T_ROWS])
    for t in (T_VECS, T_CVEC, T_WST, T_MASK, T_IDENT, T_ROWS):
        t.w = {"c": S.dsem["c"]}
    xTv = d_xT.rearrange("(k p) t -> p k t", p=128)
    for tt in range(4):
        DMA("sp", X[:, :, tt * 512:(tt + 1) * 512], xTv[:, :, tt * 512:(tt + 1) * 512], "x%d" % tt, (),
            [XT[k][tt] for k in range(8)])
    W.pump()

    MSET(EPSC[:, :], EPS, [T_EPSC])
    MSET(ONESB[:, :], 1.0 / 1024.0, [T_ONESB])
    MSET(ONEROWB[:, :], 1.0, [T_ONEROWB])
    MSET(ONEROWF, 1.0, [T_ONEROWF])
    MSET(ONECOLF, 1.0, [T_ONECOLF])
    CP(IDB[:, :], IDENT, [T_IDENT], [T_IDB])
    CP(CWB[:, :], VECS[:, C_CW:C_CW + 248], [T_VECS], [T_CWB])
    CP(BVROW[:, :], ROWS[0:1, 0:1024], [T_ROWS], [T_BVROW])
    ACT(CACT[:, :], CVEC[:, :], AF.Silu, [T_CVEC], [T_CACT])
    TT(WSM.rearrange("p (h t) -> p h t", h=8), WST.rearrange("p (h t) -> p h t", h=8),
       MASK.unsqueeze(1).broadcast_to([128, 8, 128]), ALU.mult, [T_WST, T_MASK], [T_WSM])
    CP(WSB[:, :, :], WSM.rearrange("p (h t) -> p h t", h=8), [T_WSM], [T_WSB])
    for g in range(2):
        b = PB.take()
        MM(PS[b][0:1, :], ONECOLF[:, 0:1], WSM[:, g * 512:(g + 1) * 512], True, True,
           [T_ONECOLF, T_WSM], [PT[b]])
        CP(ROWSUM[0:1, g * 512:(g + 1) * 512], PS[b][0:1, :], [PT[b]], [T_ROWSUM])
    CHf = CH[:, :, :].rearrange("p h t -> p (h t)")
    for g in range(2):
        b = PB.take()
        for hq in range(4):
            h = 4 * g + hq
            o_ = PS[b][:, hq * 128:(hq + 1) * 128]
            MM(o_, ROWS[0:1, 2048 + h * 128:2048 + (h + 1) * 128], ROWSUM[0:1, h * 128:(h + 1) * 128],
               True, False, [T_ROWS, T_ROWSUM], [PT[b]])
            MM(o_, ONEROWF[0:1, 0:128], ROWS[0:1, 1024 + h * 128:1024 + (h + 1) * 128],
               False, True, [T_ROWS, T_ONEROWF], [PT[b]])
        CP(CHf[:, g * 512:(g + 1) * 512], PS[b][:, :], [PT[b]], [T_CH])

    def ada_phase(g):
        b = PB.take(hold=True)
        for blk in range(6):
            i, s = W.acquire("ada%d_%d" % (g, blk))
            for cc in range(4):
                jl = blk * 4 + cc
                for k in range(8):
                    MM(PS[b][:, jl:jl + 1], wcol(s, k, cc * 128), CACT[:, k:k + 1], k == 0, k == 7,
                       [RTile[s], T_CACT], [PT[b]])
            W.release(i)
        TT(MOD[:, 24 * g:24 * g + 24], PS[b][:, 0:24], VECS[:, C_ADAB + 24 * g:C_ADAB + 24 * g + 24], ALU.add,
           [PT[b], T_VECS], [MODT[g]])
        PB.drop(b)
        ncol = (C_NF1, C_NMIX, C_NF2)[g]
        acol = (0, 16, 24)[g]
        STT(DER[:, acol:acol + 8], MOD[:, 24 * g + 8:24 * g + 16], 1.0, VECS[:, ncol:ncol + 8], ALU.add, ALU.mult,
            [MODT[g], T_VECS], [DERT[g]])
        if g != 1:
            gcol = (8, None, 32)[g]
            TS(DER[:, gcol:gcol + 8], MOD[:, 24 * g + 16:24 * g + 24], 0.5, None, ALU.mult, None,
               [MODT[g], DERT[g]], [DERT[g]])

    def stats_tile(tt):
        b = PB.take()
        for k in range(8):
            sq, sqt = SQ.next()
            ACT(sq, Xap(k, tt), AF.Square, [XT[k][tt]], [sqt])
            MM(PS[b][:, :], ONESB[:, :], sq, k == 0, k == 7, [sqt, T_ONESB], [PT[b]])
        r, rt = RT.next()
        RSQRT(r, PS[b][:, :], [PT[b], T_EPSC], rt)
        return r, rt

    def hmap_full(k, tt):
        return Hap(k, tt), HT[k][tt]

    def hmap_half(hh):
        return lambda k, tt: (Hap(k, tt - 2 * hh), HT[k][tt - 2 * hh])

    def apply_tile(tt, r, rt, g, hmap):
        acol = (0, 16, 24)[g]
        bcol = 24 * g
        for k in range(8):
            tm, tmt = TMPN.next()
            TT(tm, Xap(k, tt), r, ALU.mult, [XT[k][tt], rt], [tmt])
            hap, htile = hmap(k, tt)
            ACT(hap, tm, AF.Identity, [tmt, DERT[g], MODT[g]], [htile],
                bias=MOD[:, bcol + k:bcol + k + 1], scale=DER[:, acol + k:acol + k + 1])

    def norm_tile(tt, g, hmap):
        r, rt = stats_tile(tt)
        apply_tile(tt, r, rt, g, hmap)

    def ffn_phase(f, g, pre_dn7=None, dn7_hook=None):
        gcol = (8, None, 32)[g]
        sc = Scr()
        ATb = [sc.take(16384, BF16).rearrange("p (s t) -> p s t", s=4) for _ in range(2)]
        SGb = Buf([sc.take(2048, F32) for _ in range(2)])
        ATT = [[[Tile() for _ in range(4)] for _ in range(4)] for _ in range(2)]
        new_tiles = [t for a in ATT for b_ in a for t in b_] + SGb.tiles
        alias_into(new_tiles, ffn_phase.prev_tiles)
        ffn_phase.prev_tiles = new_tiles

        def GU(j):
            ig, sg_ = W.acquire("%sg%d" % (f, j))
            iu, su = W.acquire("%su%d" % (f, j))
            buf = j % 2
            order = [(hs, tt) for tt in range(4) for hs in range(4)] if j == 0 else \
                    [(hs, tt) for hs in range(4) for tt in range(4)]
            for hs, tt in order:
                bg = PB.take()
                bu = PB.take()
                for k in range(8):
                    MM(PS[bg][:, :], wcol(sg_, k, hs * 128), Hap(k, tt), k == 0, k == 7,
                       [RTile[sg_], HT[k][tt]], [PT[bg]])
                for k in range(8):
                    MM(PS[bu][:, :], wcol(su, k, hs * 128), Hap(k, tt), k == 0, k == 7,
                       [RTile[su], HT[k][tt]], [PT[bu]])
                sgap, sgt = SGb.next()
                ACT(sgap, PS[bg][:, :], AF.Silu, [PT[bg]], [sgt])
                TT(ATb[buf][:, hs, tt * 512:(tt + 1) * 512], PS[bu][:, :], sgap, ALU.mult,
                   [PT[bu], sgt], [ATT[buf][hs][tt]])
            W.release(ig)
            W.release(iu)

        def DN(j, hook=None):
            idd, sd = W.acquire("%sd%d" % (f, j))
            buf = j % 2
            for tt in range(4):
                for o in range(8):
                    b = PB.take()
                    for s in range(4):
                        MM(PS[b][:, :], RING[:, sd, s, o // 4, (o % 4) * 128:(o % 4 + 1) * 128],
                           ATb[buf][:, s, tt * 512:(tt + 1) * 512], s == 0, s == 3,
                           [RTile[sd], ATT[buf][s][tt]], [PT[b]])
                    STT(Xap(o, tt), PS[b][:, :], DER[:, gcol + o:gcol + o + 1], Xap(o, tt), ALU.mult, ALU.add,
                        [PT[b], XT[o][tt], DERT[g]], [XT[o][tt]])
                if hook is not None:
                    hook(tt)
            W.release(idd)

        GU(0)
        for j in range(1, 8):
            GU(j)
            DN(j - 1)
        if pre_dn7 is not None:
            pre_dn7()
        DN(7, dn7_hook)

    ffn_phase.prev_tiles = setup_tiles

    mix_state = {}

    def mixer_half(hh, first, pre_m7, m7_hooks):
        sc = Scr()
        UZ = sc.take(8 * 1056 * 2, BF16).rearrange("p (o t) -> p o t", o=8)
        SGb = Buf([sc.take(2048, F32) for _ in range(2)])
        wbase = sc.off
        VGb = Buf([sc.take(4096, F32) for _ in range(2)])
        VNb = Buf([sc.take(2048, BF16) for _ in range(2)])
        TMPV = sc.take(4096, F32).rearrange("p (h t) -> p h t", h=8)
        T_TMPV = Tile()
        sc.off = wbase
        C32 = sc.take(16384, F32).rearrange("p (o t) -> p o t", o=8)
        C32T = [Tile() for _ in range(8)]
        m2_tiles = VGb.tiles + VNb.tiles + [T_TMPV]
        m5_tiles = C32T
        if first:
            UZT = [[Tile() for _ in range(2)] for _ in range(8)]
            ZHT = Tile()
            alias_into([t for a in UZT for t in a] + [ZHT] + SGb.tiles + m2_tiles, ffn_phase.prev_tiles)
            mix_state["UZT"] = UZT
            mix_state["ZHT"] = ZHT
            mix_state["SGt"] = SGb.tiles
            MSET(UZ[:, :, 0:32], 0.0, [ZHT])
        else:
            UZT = mix_state["UZT"]
            ZHT = mix_state["ZHT"]
            SGb.tiles = mix_state["SGt"]
            alias_into(m2_tiles, mix_state["m5"])
        mix_state["m5"] = m5_tiles

        def Uap(o, tl):
            return UZ[:, o, 32 + tl * 512:32 + (tl + 1) * 512]

        def bin_col(part, o):
            c = C_BIN + part * 8 + o
            return VECS[:, c:c + 1]

        for blk in range(2):
            i, s = W.acquire("m%din%d" % (hh, blk))
            for oc in range(4):
                o = blk * 4 + oc
                for tl in range(2):
                    b = PB.take()
                    for k in range(8):
                        MM(PS[b][:, :], wcol(s, k, oc * 128), Hap(k, tl), k == 0, k == 7,
                           [RTile[s], HT[k][tl]], [PT[b]])
                    ACT(Uap(o, tl), PS[b][:, :], AF.Gelu_apprx_tanh, [PT[b], T_VECS], [UZT[o][tl]],
                        bias=bin_col(0, o))
            W.release(i)

        i2, s2 = W.acquire("m%din2" % hh)
        i3, s3 = W.acquire("m%din3" % hh)
        cst = {}

        def proj_pe(tc):
            tl = tc // 4
            c0 = tl * 512 + (tc % 4) * 128
            bb = []
            for nh in range(2):
                s = (s2, s3)[nh]
                b = PB.take()
                bb.append(b)
                for k in range(8):
                    MM(PS[b][:, :], H[:, k, c0:c0 + 128], RING[:, s, k // 2, k % 2, :], k == 0, False,
                       [HT[k][tl], RTile[s]], [PT[b]])
                MM(PS[b][:, :], ONEROWB[0:1, :], BVROW[0:1, nh * 512:(nh + 1) * 512], False, True,
                   [T_ONEROWB, T_BVROW], [PT[b]])
            cst[tc] = {"bb": bb, "tl": tl, "c0": c0}

        def proj_act(tc):
            d = cst[tc]
            vg, vgt = VGb.next()
            st_ap, st_t = STATSB.next()
            for nh in range(2):
                b = d["bb"][nh]
                ACT(vg[:, nh * 512:(nh + 1) * 512], PS[b][:, :], AF.Gelu_apprx_tanh, [PT[b]], [vgt])
                S.emit("dve", lambda e, o=st_ap[:, nh * 6:(nh + 1) * 6], i_=vg[:, nh * 512:(nh + 1) * 512]:
                       e.bn_stats(out=o, in_=i_), [vgt], [st_t])
            d.update(vg=vg, vgt=vgt, st_ap=st_ap, st_t=st_t)

        def chain(tc):
            d = cst[tc]
            mv, mvt = MVTB.next()
            S.emit("dve", lambda e, o=mv[:, 0:2], i_=d["st_ap"]: e.bn_aggr(out=o, in_=i_), [d["st_t"]], [mvt])
            RSQRT(mv[:, 2:3], mv[:, 1:2], [mvt, T_EPSC], mvt)
            vn, vnt = VNb.next()
            TS(vn, d["vg"], mv[:, 0:1], mv[:, 2:3], ALU.subtract, ALU.mult, [d["vgt"], mvt], [vnt])
            d.update(vn=vn, vnt=vnt)

        def spatial(tc):
            d = cst.pop(tc)
            tl, c0, vn, vnt = d["tl"], d["c0"], d["vn"], d["vnt"]
            b2 = [PB.take(), PB.take()]
            for h in range(8):
                MM(PS[b2[h // 4]][:, (h % 4) * 128:(h % 4 + 1) * 128], vn[:, h * 128:(h + 1) * 128], WSB[:, h, :],
                   True, True, [vnt, T_WSB], [PT[b2[h // 4]]])
            for h in range(8):
                STT(TMPV[:, h, :], PS[b2[h // 4]][:, (h % 4) * 128:(h % 4 + 1) * 128],
                    VECS[:, C_SLG + h:C_SLG + h + 1], CH[:, h, :], ALU.mult, ALU.add,
                    [PT[b2[h // 4]], T_VECS, T_CH], [T_TMPV])
            ucols = UZ[:, :, 32 + c0:32 + c0 + 128]
            ut = [UZT[o][tl] for o in range(8)]
            TT(ucols, TMPV, ucols, ALU.mult, [T_TMPV] + ut, ut)

        proj_pe(0)
        proj_act(0)
        for tc in range(8):
            if tc + 1 < 8:
                proj_pe(tc + 1)
            chain(tc)
            if tc + 1 < 8:
                proj_act(tc + 1)
            spatial(tc)
        W.release(i2)
        W.release(i3)

        for g in range(2):
            ig, sgx = W.acquire("m%din%d" % (hh, 8 + g))
            ia, sa = W.acquire("m%dwa%d" % (hh, g))
            for oc in range(4):
                o = g * 4 + oc
                for tl in range(2):
                    bA = PB.take()
                    bB = PB.take()
                    for k in range(8):
                        MM(PS[bA][:, :], wcol(sgx, k, oc * 128), Hap(k, tl), k == 0, k == 7,
                           [RTile[sgx], HT[k][tl]], [PT[bA]])
                    for k in range(8):
                        MM(PS[bB][:, :], wcol(sa, k, oc * 128), Uap(k, tl), k == 0, k == 7,
                           [RTile[sa], UZT[k][tl]], [PT[bB]])
                    sgap, sgt = SGb.next()
                    ACT(sgap, PS[bA][:, :], AF.Sigmoid, [PT[bA], T_VECS], [sgt], bias=bin_col(4, o))
                    TT(Hap(o, 2 + tl), PS[bB][:, :], sgap, ALU.mult, [PT[bB], sgt], [HT[o][2 + tl]])
            W.release(ig)
            W.release(ia)

        W.reserve = 2
        for g in range(2):
            iv, sv = W.acquire("m%din%d" % (hh, 4 + g))
            ic, scx = W.acquire("m%din%d" % (hh, 6 + g))
            for oc in range(4):
                o = g * 4 + oc
                for tl in range(2):
                    bA = PB.take()
                    bB = PB.take()
                    for k in range(8):
                        MM(PS[bA][:, :], wcol(sv, k, oc * 128), Hap(k, tl), k == 0, k == 7,
                           [RTile[sv], HT[k][tl]], [PT[bA]])
                    for k in range(8):
                        MM(PS[bB][:, :], wcol(scx, k, oc * 128), Hap(k, tl), k == 0, k == 7,
                           [RTile[scx], HT[k][tl]], [PT[bB]])
                    sgap, sgt = SGb.next()
                    ACT(sgap, PS[bB][:, :], AF.Sigmoid, [PT[bB], T_VECS], [sgt], bias=bin_col(3, o))
                    STT(Uap(o, tl), PS[bA][:, :], bin_col(2, o), sgap, ALU.add, ALU.mult,
                        [PT[bA], sgt, T_VECS], [UZT[o][tl]])
            W.release(iv)
            W.release(ic)

        alias_into(m5_tiles, m2_tiles)
        if hh == 0:
            CP(HALO[:, :, 2:32], UZ[:, :, 32 + 994:32 + 1024], [UZT[o][1] for o in range(8)], [T_HALO])
        else:
            CP(UZ[:, :, 2:32], HALO[:, :, 2:32], [T_HALO], [ZHT])
        ctiles = [(1, o) for o in range(8)] + [(0, o) for o in range(8)]
        bmq = {}
        dg = {}
        evs = {}
        chn = {}

        def build_dg(idx):
            st, o = ctiles[idx]
            sl = W.take_raw()
            DG = RING[:, sl].rearrange("p a b n -> p (a b n)")[:, 0:3968].rearrange("p (k j) -> p k j", k=31)
            TT(DG, IDB[:, :].unsqueeze(1).broadcast_to([128, 31, 128]),
               CWB[:, o * 31:(o + 1) * 31].unsqueeze(2).broadcast_to([128, 31, 128]), ALU.mult,
               [T_IDB, T_CWB], [RTile[sl]])
            dg[idx] = (sl, DG)

        def stats_mm(idx):
            st, o = ctiles[idx]
            bm, bq = bmq[st]
            cb, cbt, cq, cqt = evs.pop(idx)
            MM(PS[bm][:, :], ONESB[:, :], cb, o == 0, o == 7, [cbt, T_ONESB], [PT[bm]])
            MM(PS[bq][:, :], ONESB[:, :], cq, o == 0, o == 7, [cqt, T_ONESB], [PT[bq]])

        def stat_chain(st):
            bm, bq = bmq[st]
            mean_sb, mean_t = RT.next()
            rs, rs_t = RT.next()
            tq, tq_t = TMPN.next()
            mr, mr_t = TMPN.next()
            CP(mean_sb, PS[bm][:, :], [PT[bm]], [mean_t])
            TT(tq, mean_sb, mean_sb, ALU.mult, [mean_t], [tq_t])
            TT(tq, PS[bq][:, :], tq, ALU.subtract, [PT[bq], tq_t], [tq_t])
            RSQRT(rs, tq, [tq_t, T_EPSC], rs_t)
            TT(mr, mean_sb, rs, ALU.mult, [mean_t, rs_t], [mr_t])
            PB.drop(bm)
            PB.drop(bq)
            chn[st] = (rs, rs_t, mr, mr_t)

        def norm_o(st, o):
            rs, rs_t, mr, mr_t = chn[st]
            TT(C32[:, o, :], C32[:, o, :], rs, ALU.mult, [C32T[o], rs_t], [C32T[o]])
            TT(C32[:, o, :], C32[:, o, :], mr, ALU.subtract, [C32T[o], mr_t], [C32T[o]])
            ACT(Uap(o, st), C32[:, o, :], AF.Silu, [C32T[o], T_VECS], [UZT[o][st]],
                bias=VECS[:, C_CLB + o:C_CLB + o + 1], scale=VECS[:, C_CLG + o:C_CLG + o + 1])

        build_dg(0)
        for idx, (st, o) in enumerate(ctiles):
            if o == 0:
                bmq[st] = (PB.take(hold=True), PB.take(hold=True))
            sl, DG = dg.pop(idx)
            b = PB.take()
            rds = [RTile[sl], UZT[o][st], (UZT[o][st - 1] if st > 0 else ZHT)]
            for k in range(31):
                cc = 2 + st * 512 + k
                MM(PS[b][:, :], DG[:, k, :], UZ[:, o, cc:cc + 512], k == 0, k == 30, rds, [PT[b]])
            W.give_raw(sl)
            if idx + 1 < 16:
                build_dg(idx + 1)
            if idx > 0:
                stats_mm(idx - 1)
                if ctiles[idx - 1] == (1, 7):
                    stat_chain(1)
            if st == 0 and o % 4 == 0:
                for oo in range(o, o + 4):
                    norm_o(1, oo)
            cvb = VECS[:, C_CVB + o:C_CVB + o + 1]
            ACT(C32[:, o, :], PS[b][:, :], AF.Identity, [PT[b], T_VECS], [C32T[o]], bias=cvb)
            cb, cbt = SQ.next()
            cq, cqt = SQ.next()
            ACT(cb, PS[b][:, :], AF.Identity, [PT[b], T_VECS], [cbt], bias=cvb)
            ACT(cq, PS[b][:, :], AF.Square, [PT[b], T_VECS], [cqt], bias=cvb)
            evs[idx] = (cb, cbt, cq, cqt)
        stats_mm(15)
        stat_chain(0)
        for o in range(8):
            norm_o(0, o)
        W.reserve = 0
        W.pump()

        m6 = []
        for g in range(2):
            ig, sgx = W.acquire("m%din%d" % (hh, 10 + g))
            ib, sbx = W.acquire("m%dwb%d" % (hh, g))
            m6.append((ig, sgx, ib, sbx))
        for tl in (1, 0):
            for g in range(2):
                ig, sgx, ib, sbx = m6[g]
                for oc in range(4):
                    o = g * 4 + oc
                    bA = PB.take()
                    bB = PB.take()
                    for k in range(8):
                        MM(PS[bA][:, :], wcol(sgx, k, oc * 128), Hap(k, tl), k == 0, k == 7,
                           [RTile[sgx], HT[k][tl]], [PT[bA]])
                    for k in range(8):
                        MM(PS[bB][:, :], wcol(sbx, k, oc * 128), Uap(k, tl), k == 0, k == 7,
                           [RTile[sbx], UZT[k][tl]], [PT[bB]])
                    sgap, sgt = SGb.next()
                    ACT(sgap, PS[bA][:, :], AF.Sigmoid, [PT[bA], T_VECS], [sgt], bias=bin_col(5, o))
                    tm, tmt = TMPN.next()
                    TT(tm, PS[bB][:, :], sgap, ALU.mult, [PT[bB], sgt], [tmt])
                    TT(Hap(o, 2 + tl), Hap(o, 2 + tl), tm, ALU.add, [HT[o][2 + tl], tmt], [HT[o][2 + tl]])
        for ig, sgx, ib, sbx in m6:
            W.release(ig)
            W.release(ib)

        if pre_m7 is not None:
            pre_m7()

        ti = 0
        for g in range(2):
            io, so = W.acquire("m%dwo%d" % (hh, g))
            for oc in range(4):
                o = g * 4 + oc
                for tl in range(2):
                    if ti in m7_hooks:
                        m7_hooks[ti]()
                    ti += 1
                    b = PB.take()
                    for k in range(8):
                        MM(PS[b][:, :], wcol(so, k, oc * 128), Hap(k, 2 + tl), k == 0, k == 7,
                           [RTile[so], HT[k][2 + tl]], [PT[b]])
                    tt = 2 * hh + tl
                    STT(Xap(o, tt), PS[b][:, :], MOD[:, 40 + o:41 + o], Xap(o, tt), ALU.mult, ALU.add,
                        [PT[b], XT[o][tt], MODT[1]], [XT[o][tt]])
            W.release(io)
        return [t for a in UZT for t in a] + [ZHT] + SGb.tiles + m2_tiles + m5_tiles

    outv = d_out.rearrange("(k p) t -> p k t", p=128)
    T_OUT = Tile()

    def final_tile(tt):
        r, rt = stats_tile(tt)
        for k in range(8):
            STT(Xap(k, tt), Xap(k, tt), VECS[:, C_NFIN + k:C_NFIN + k + 1], r, ALU.mult, ALU.mult,
                [XT[k][tt], rt, T_VECS], [XT[k][tt]])
        DMA("sp", outv[:, :, tt * 512:(tt + 1) * 512], X[:, :, tt * 512:(tt + 1) * 512], "out",
            [XT[k][tt] for k in range(8)], [T_OUT])

    r0 = stats_tile(0)
    r1 = stats_tile(1)
    ada_phase(0)
    apply_tile(0, r0[0], r0[1], 0, hmap_full)
    apply_tile(1, r1[0], r1[1], 0, hmap_full)
    norm_tile(2, 0, hmap_full)
    norm_tile(3, 0, hmap_full)

    def hook1(tt):
        if tt == 1:
            norm_tile(0, 1, hmap_half(0))
        elif tt == 2:
            norm_tile(1, 1, hmap_half(0))

    ffn_phase("f1", 0, pre_dn7=lambda: ada_phase(1), dn7_hook=hook1)

    hst = {}

    def mk_hooks(tts, g, hmap):
        def st_(tt):
            hst[tt] = stats_tile(tt)

        def ap_(tt):
            r, rt = hst.pop(tt)
            apply_tile(tt, r, rt, g, hmap)
        return {0: lambda: st_(tts[0]), 4: lambda: ap_(tts[0]), 8: lambda: st_(tts[1]), 12: lambda: ap_(tts[1])}

    mixer_half(0, True, None, mk_hooks((2, 3), 1, hmap_half(1)))
    mtiles = mixer_half(1, False, lambda: ada_phase(2), mk_hooks((0, 1), 2, hmap_full))
    ffn_phase.prev_tiles = mtiles
    norm_tile(2, 2, hmap_full)
    norm_tile(3, 2, hmap_full)

    def hook2(tt):
        if tt >= 1:
            final_tile(tt - 1)

    ffn_phase("f2", 2, dn7_hook=hook2)
    final_tile(3)
    T_OUT.w = {"out": S.dsem["out"]}
    S.emit("sp", None, [T_OUT], ())

    assert W.cur == len(blocks) and W.nxt == len(blocks), (W.cur, W.nxt, len(blocks))

    sig = {e: set() for e in ("pe", "act", "dve", "pool")}
    for e in S.ENG:
        for fn, waits, tok in S.q[e]:
            for k, v in waits:
                if k in sig:
                    sig[k].add(v)
    rank = {}
    for e, st_ in sig.items():
        rank[e] = {v: i + 1 for i, v in enumerate(sorted(st_))}

    def replay(ename, handle):
        for fn, waits, tok in S.q[ename]:
            for k, v in waits:
                if k in sem_eng:
                    handle.wait_ge(sem_eng[k], rank[k][v])
                else:
                    handle.wait_ge(sem_dma[k], v)
            if fn is None:
                continue
            inst = fn(handle)
            if tok[0] in sem_dma:
                inst.then_inc(sem_dma[tok[0]], 16)
            elif tok[1] in sig[tok[0]]:
                inst.then_inc(sem_eng[tok[0]], 1)

    with nc.Block() as block:
        @block.sync
        def _(e):
            replay("sp", e)

        @block.gpsimd
        def _(e):
            replay("pool", e)

        @block.scalar
        def _(e):
            replay("act", e)

        @block.vector
        def _(e):
            replay("dve", e)

        @block.tensor
        def _(e):
            replay("pe", e)

    es.close()
    return nc


def prep_inputs(inp):
    g = lambda n: np.asarray(inp[n], dtype=np.float32)
    x = g("x")
    c = g("c")
    vecs = np.zeros((128, NV), np.float32)

    def put(col, v):
        v = np.asarray(v, np.float32).reshape(-1, 128).T
        vecs[:, col:col + v.shape[1]] = v

    put(C_NF1, g("norm_ffn1")[0])
    put(C_NMIX, g("norm_mix")[0])
    put(C_NF2, g("norm_ffn2")[0])
    put(C_NFIN, g("norm_final"))
    put(C_BIN, g("mix_b_in")[0])
    put(C_SLG, g("sgu_ln_g")[0])
    put(C_SLB, g("sgu_ln_b")[0])
    put(C_CVB, g("conv_b")[0])
    put(C_CLG, g("conv_ln_g")[0])
    put(C_CLB, g("conv_ln_b")[0])
    put(C_ADAB, g("ada_b")[0])
    cw = g("conv_w")[0]
    for o in range(8):
        vecs[:, C_CW + o * 31:C_CW + (o + 1) * 31] = cw[:, o * 128:(o + 1) * 128].T
    wsT = np.ascontiguousarray(g("sgu_w_s")[0].transpose(2, 0, 1))
    s_idx = np.arange(128)
    mask = (s_idx[:, None] <= s_idx[None, :]).astype(np.float32)
    ident = np.eye(128, dtype=np.float32)
    rows = np.concatenate([g("mix_b_in")[0][1024:2048], g("sgu_b_s")[0].reshape(-1), g("sgu_ln_b")[0]])[None, :]
    rows = np.ascontiguousarray(rows, dtype=np.float32)
    shared = {
        "vecs": vecs, "wsT": wsT, "mask": mask, "ident": ident, "rows": rows,
        "ada_w": np.ascontiguousarray(g("ada_w")[0]),
        "f1g": np.ascontiguousarray(g("ffn1_w_gate")[0]), "f1u": np.ascontiguousarray(g("ffn1_w_up")[0]),
        "f1d": np.ascontiguousarray(g("ffn1_w_down")[0]),
        "f2g": np.ascontiguousarray(g("ffn2_w_gate")[0]), "f2u": np.ascontiguousarray(g("ffn2_w_up")[0]),
        "f2d": np.ascontiguousarray(g("ffn2_w_down")[0]),
        "w_in": np.ascontiguousarray(g("mix_w_in")[0]),
        "w_a": np.ascontiguousarray(g("w_branch_a")[0]), "w_b": np.ascontiguousarray(g("w_branch_b")[0]),
        "w_o": np.ascontiguousarray(g("w_out")[0]),
    }
    in_maps = []
    for b in range(8):
        m = dict(shared)
        m["xT"] = np.ascontiguousarray(x[b].T)
        m["cvec"] = np.ascontiguousarray(c[b].reshape(8, 128).T)
        in_maps.append(m)
    return in_maps


def kernel(**inputs):
    in_maps = prep_inputs(inputs)
    nc = build_program(STAGE if STAGE < 9 else 4)
    res = run_bass_kernel_spmd(nc, in_maps, core_ids=list(range(8)))
    out = np.stack([np.asarray(r["outT"], dtype=np.float32).T for r in res.results], axis=0)
    return np.ascontiguousarray(out)
```

```python
import os
from contextlib import ExitStack

import numpy as np
import concourse.bass as bass
import concourse.mybir as mybir
from concourse.bass_utils import run_bass_kernel_spmd

F32 = mybir.dt.float32
BF16 = mybir.dt.bfloat16
AF = mybir.ActivationFunctionType
ALU = mybir.AluOpType

EPS = 1e-6
NS = 6
NV = 440
STAGE = int(os.environ.get("KSTAGE", "9"))

C_NF1, C_NMIX, C_NF2, C_NFIN = 0, 8, 16, 24
C_BIN = 32
C_SLG, C_SLB = 80, 88
C_CVB, C_CLG, C_CLB = 96, 104, 112
C_ADAB = 120
C_CW = 192


class Tile:
    __slots__ = ("w", "r")

    def __init__(self):
        self.w = {}
        self.r = {}


def alias_into(new_tiles, old_tiles):
    w, r = {}, {}
    for t in old_tiles:
        for k, v in t.w.items():
            if w.get(k, 0) < v:
                w[k] = v
        for k, v in t.r.items():
            if r.get(k, 0) < v:
                r[k] = v
    for t in new_tiles:
        t.w = dict(w)
        t.r = dict(r)


class Sched:
    ENG = ("pe", "act", "dve", "pool", "sp")

    def __init__(self):
        self.q = {e: [] for e in self.ENG}
        self.seq = {e: 0 for e in self.ENG}
        self.known = {e: {} for e in self.ENG}
        self.dsem = {}

    def emit(self, eng, fn, reads=(), writes=(), dma=None):
        raw, oth = {}, {}
        for t in reads:
            for k, v in t.w.items():
                if raw.get(k, 0) < v:
                    raw[k] = v
        for t in writes:
            for k, v in t.w.items():
                if oth.get(k, 0) < v:
                    oth[k] = v
            for k, v in t.r.items():
                if oth.get(k, 0) < v:
                    oth[k] = v
        waits = []
        kn = self.known[eng]
        for k, v in raw.items():
            if k == eng and eng == "pe":
                continue
            if kn.get(k, 0) >= v:
                continue
            kn[k] = v
            waits.append((k, v))
        for k, v in oth.items():
            if k == eng:
                continue
            if kn.get(k, 0) >= v:
                continue
            kn[k] = v
            waits.append((k, v))
        if dma is None:
            self.seq[eng] += 1
            tok = (eng, self.seq[eng])
        else:
            self.dsem[dma] = self.dsem.get(dma, 0) + 16
            tok = (dma, self.dsem[dma])
        self.q[eng].append((fn, waits, tok))
        for t in reads:
            if t.r.get(tok[0], 0) < tok[1]:
                t.r[tok[0]] = tok[1]
        for t in writes:
            t.w = {tok[0]: tok[1]}
            t.r = {}
        return tok


class Banks:
    def __init__(self):
        self.next = 0
        self.held = set()

    def take(self, hold=False):
        for _ in range(16):
            b = self.next
            self.next = (self.next + 1) % 8
            if b not in self.held:
                if hold:
                    self.held.add(b)
                return b
        raise RuntimeError("no psum bank")

    def drop(self, b):
        self.held.discard(b)


class Buf:
    def __init__(self, aps):
        self.aps = aps
        self.tiles = [Tile() for _ in aps]
        self.i = 0

    def next(self):
        j = self.i % len(self.aps)
        self.i += 1
        return self.aps[j], self.tiles[j]


def build_program(stage=STAGE):
    nc = bass.Bass("TRN2", target_bir_lowering=False)
    S = Sched()
    PB = Banks()
    es = ExitStack()

    def din(name, shape):
        return nc.dram_tensor(name, list(shape), F32, kind="ExternalInput").ap()

    d_xT = din("xT", [1024, 2048])
    d_cvec = din("cvec", [128, 8])
    d_vecs = din("vecs", [128, NV])
    d_wsT = din("wsT", [128, 8, 128])
    d_mask = din("mask", [128, 128])
    d_ident = din("ident", [128, 128])
    d_rows = din("rows", [1, 3072])
    d_ada = din("ada_w", [1024, 9216])
    d_f = {}
    for f in ("f1", "f2"):
        d_f[f + "g"] = din(f + "g", [1024, 4096])
        d_f[f + "u"] = din(f + "u", [1024, 4096])
        d_f[f + "d"] = din(f + "d", [4096, 1024])
    d_win = din("w_in", [1024, 6144])
    d_wa = din("w_a", [1024, 1024])
    d_wb = din("w_b", [1024, 1024])
    d_wo = din("w_o", [1024, 1024])
    d_out = nc.dram_tensor("outT", [1024, 2048], F32, kind="ExternalOutput").ap()

    def sb(name, shape, dt):
        return es.enter_context(nc.sbuf_tensor(name, list(shape), dt))

    X = sb("X", [128, 8, 2048], F32)
    H = sb("H", [128, 8, 2048], BF16)
    RING = sb("RING", [128, NS, 4, 2, 512], BF16)
    VECS = sb("VECS", [128, NV], F32)
    MOD = sb("MOD", [128, 72], F32)
    DER = sb("DER", [128, 40], F32)
    CVEC = sb("CVEC", [128, 8], F32)
    CACT = sb("CACT", [128, 8], BF16)
    ONESB = sb("ONESB", [128, 128], BF16)
    IDB = sb("IDB", [128, 128], BF16)
    WSB = sb("WSB", [128, 8, 128], BF16)
    CH = sb("CH", [128, 8, 128], F32)
    BVROW = sb("BVROW", [1, 1024], BF16)
    ONEROWB = sb("ONEROWB", [1, 128], BF16)
    CWB = sb("CWB", [128, 248], BF16)
    HALO = sb("HALO", [128, 8, 32], BF16)
    RTb = sb("RTb", [128, 2, 512], F32)
    SQb = sb("SQb", [128, 4, 512], BF16)
    TMPNb = sb("TMPNb", [128, 2, 512], F32)
    STATS = sb("STATS", [128, 2, 12], F32)
    MVT = sb("MVT", [128, 2, 4], F32)
    EPSC = sb("EPSC", [128, 8], F32)
    SCRN = 18688
    SCR = sb("SCR", [128, SCRN], BF16)

    PS = [es.enter_context(nc.psum_tensor("ps%d" % i, [128, 512], F32)) for i in range(8)]
    PT = [Tile() for _ in range(8)]

    sem_eng = {e: es.enter_context(nc.semaphore("s_" + e)) for e in ("pe", "act", "dve", "pool")}
    dma_keys = ["c", "out"] + ["x%d" % i for i in range(4)] + ["w%d" % i for i in range(NS)]
    sem_dma = {k: es.enter_context(nc.semaphore("d_" + k)) for k in dma_keys}

    def MM(out, lhsT, rhs, start, stop, reads, writes):
        S.emit("pe", lambda e, o=out, l=lhsT, r=rhs, a=start, b=stop:
               e.matmul(o, lhsT=l, rhs=r, start=a, stop=b), reads, writes)

    def ACT(out, in_, func, reads, writes, bias=None, scale=None):
        kw = {}
        if bias is not None:
            kw["bias"] = bias
        if scale is not None:
            kw["scale"] = scale
        S.emit("act", lambda e, o=out, i=in_, f=func, kw=kw:
               e.activation(out=o, in_=i, func=f, **kw), reads, writes)

    def TT(out, in0, in1, op, reads, writes, eng="dve"):
        S.emit(eng, lambda e, o=out, a=in0, b=in1, p=op:
               e.tensor_tensor(out=o, in0=a, in1=b, op=p), reads, writes)

    def TS(out, in0, s1, s2, op0, op1, reads, writes, eng="dve"):
        if op1 is None:
            S.emit(eng, lambda e, o=out, a=in0, x=s1, p=op0:
                   e.tensor_scalar(out=o, in0=a, scalar1=x, scalar2=None, op0=p), reads, writes)
        else:
            S.emit(eng, lambda e, o=out, a=in0, x=s1, y=s2, p=op0, q=op1:
                   e.tensor_scalar(out=o, in0=a, scalar1=x, scalar2=y, op0=p, op1=q), reads, writes)

    def STT(out, in0, scalar, in1, op0, op1, reads, writes):
        S.emit("dve", lambda e, o=out, a=in0, s=scalar, b=in1, p=op0, q=op1:
               e.scalar_tensor_tensor(out=o, in0=a, scalar=s, in1=b, op0=p, op1=q), reads, writes)

    def RSQRT(out, in_, reads, wtile):
        ACT(out, in_, AF.Sqrt, reads, [wtile], bias=EPSC[:in_.shape[0], 0:1])
        S.emit("dve", lambda e, o=out: e.reciprocal(out=o, in_=o), [wtile], [wtile])

    I32 = mybir.dt.int32

    def RSQRT_DVE(y, a, var, reads, wtile):
        TS(a, var, EPS, None, ALU.add, None, reads, [wtile])
        TS(y.bitcast(I32), a.bitcast(I32), -0.5, 1597463007.0, ALU.mult, ALU.add, [wtile], [wtile])
        for _ in range(3):
            TS(var, y, y, a, ALU.mult, ALU.mult, [wtile], [wtile])
            TS(var, var, -0.5, 1.5, ALU.mult, ALU.add, [wtile], [wtile])
            TS(y, y, var, None, ALU.mult, None, [wtile], [wtile])

    def CP(out, in_, reads, writes, eng="dve"):
        S.emit(eng, lambda e, o=out, i=in_: e.tensor_copy(out=o, in_=i), reads, writes)

    def MSET(ap, val, writes, eng="dve"):
        S.emit(eng, lambda e, a=ap, v=val: e.memset(a, v), (), writes)

    def DMA(q, out, in_, key, reads, writes):
        S.emit(q, lambda e, o=out, i=in_: e.dma_start(out=o, in_=i), reads, writes, dma=key)

    XT = [[Tile() for _ in range(4)] for _ in range(8)]
    HT = [[Tile() for _ in range(4)] for _ in range(8)]
    RTile = [Tile() for _ in range(NS)]
    T_VECS, T_CVEC, T_CACT, T_ONESB, T_IDB, T_WSB, T_CH = (Tile() for _ in range(7))
    T_BVROW, T_ONEROWB, T_CWB, T_HALO, T_EPSC = (Tile() for _ in range(5))
    MODT = [Tile() for _ in range(4)]
    DERT = [Tile() for _ in range(4)]
    RT = Buf([RTb[:, i, :] for i in range(2)])
    SQ = Buf([SQb[:, i, :] for i in range(4)])
    TMPN = Buf([TMPNb[:, i, :] for i in range(2)])
    STATSB = Buf([STATS[:, i, :] for i in range(2)])
    MVTB = Buf([MVT[:, i, :] for i in range(2)])

    def Xap(k, tt):
        return X[:, k, tt * 512:(tt + 1) * 512]

    def Hap(k, tt):
        return H[:, k, tt * 512:(tt + 1) * 512]

    class Scr:
        def __init__(self):
            self.off = 0

        def take(self, nbytes, dt, parts=None):
            assert self.off % 32 == 0
            a = self.off // 2
            n2 = nbytes // 2
            assert a + n2 <= SCRN, (a, n2)
            v = SCR[:, a:a + n2] if parts is None else SCR[0:parts, a:a + n2]
            self.off += (nbytes + 31) // 32 * 32
            if dt is F32:
                v = v.bitcast(F32)
            return v

    def colblock(Wd_, c0):
        return Wd_.rearrange("(a b p) n -> p a b n", a=4, b=2, p=128)[:, :, :, c0:c0 + 512]

    def rowblock(Wd_, j):
        return Wd_.rearrange("(s p) (c n) -> p s c n", p=128, c=2)[:, 4 * j:4 * j + 4]

    blocks = []

    def add_ada(g, blks=range(6)):
        for blk in blks:
            blocks.append(("ada%d_%d" % (g, blk), colblock(d_ada, 3072 * g + 512 * blk)))

    def add_ffn(f, before_last=None, before_first_dn=None):
        def gu(j):
            blocks.append(("%sg%d" % (f, j), colblock(d_f[f + "g"], 512 * j)))
            blocks.append(("%su%d" % (f, j), colblock(d_f[f + "u"], 512 * j)))

        def dn(j):
            blocks.append(("%sd%d" % (f, j), rowblock(d_f[f + "d"], j)))
        gu(0)
        for j in range(1, 8):
            gu(j)
            if j == 1 and before_first_dn is not None:
                before_first_dn()
            dn(j - 1)
        dn(7)
        if before_last is not None:
            before_last()

    def add_mixer(hh, before_wo=None):
        def inb(i):
            blocks.append(("m%din%d" % (hh, i), colblock(d_win, 512 * i)))
        for i in (0, 1, 2, 3):
            inb(i)
        for g in range(2):
            inb(8 + g)
            blocks.append(("m%dwa%d" % (hh, g), colblock(d_wa, 512 * g)))
        for g in range(2):
            inb(4 + g)
            inb(6 + g)
        for g in range(2):
            inb(10 + g)
            blocks.append(("m%dwb%d" % (hh, g), colblock(d_wb, 512 * g)))
        if before_wo is not None:
            before_wo()
        for g in range(2):
            blocks.append(("m%dwo%d" % (hh, g), colblock(d_wo, 512 * g)))

    add_ada(0, range(4))
    add_ffn("f1", before_last=lambda: add_ada(1), before_first_dn=lambda: add_ada(0, (4, 5)))
    add_mixer(0)
    add_mixer(1, before_wo=lambda: add_ada(2))
    add_ffn("f2")

    class WS:
        def __init__(self):
            self.nxt = 0
            self.cur = 0
            self.free = list(range(NS))
            self.slot_of = {}
            self.reserve = 0

        def pump(self):
            while self.nxt < len(blocks) and len(self.free) > self.reserve:
                s = self.free.pop(0)
                i = self.nxt
                self.nxt += 1
                DMA("pool", RING[:, s], blocks[i][1], "w%d" % s, (), [RTile[s]])
                self.slot_of[i] = s

        def acquire(self, name):
            i = self.cur
            self.cur += 1
            assert blocks[i][0] == name, (blocks[i][0], name)
            if i not in self.slot_of:
                self.pump()
            assert i in self.slot_of, ("weight block not issued", name)
            return i, self.slot_of[i]

        def release(self, i):
            s = self.slot_of.pop(i)
            self.free.append(s)
            self.pump()

        def take_raw(self):
            assert self.free, "no free ring slot for raw use"
            return self.free.pop(0)

        def give_raw(self, s):
            self.free.append(s)
            self.pump()

    W = WS()

    def wcol(s, k, c0):
        return RING[:, s, k // 2, k % 2, c0:c0 + 128]

    scr = Scr()
    WST = scr.take(4096, F32)
    WSM = scr.take(4096, F32)
    MASK = scr.take(512, F32)
    IDENT = scr.take(512, F32)
    ONECOLF = scr.take(32, F32)
    ROWS = scr.take(12288, F32, parts=1)
    ROWSUM = scr.take(4096, F32, parts=1)
    ONEROWF = scr.take(512, F32, parts=1)
    T_WST, T_WSM, T_MASK, T_IDENT, T_ONECOLF, T_ROWS, T_ROWSUM, T_ONEROWF = (Tile() for _ in range(8))
    setup_tiles = [T_WST, T_WSM, T_MASK, T_IDENT, T_ONECOLF, T_ROWS, T_ROWSUM, T_ONEROWF]

    DMA("sp", VECS[:, :], d_vecs, "c", (), [T_VECS])
    DMA("sp", CVEC[:, :], d_cvec, "c", (), [T_CVEC])
    DMA("sp", WST.rearrange("p (h t) -> p h t", h=8), d_wsT, "c", (), [T_WST])
    DMA("sp", MASK, d_mask, "c", (), [T_MASK])
    DMA("sp", IDENT, d_ident, "c", (), [T_IDENT])
    DMA("sp", ROWS, d_rows, "c", (), [T_ROWS])
    for t in (T_VECS, T_CVEC, T_WST, T_MASK, T_IDENT, T_ROWS):
        t.w = {"c": S.dsem["c"]}
    xTv = d_xT.rearrange("(k p) t -> p k t", p=128)
    for tt in range(4):
        if tt == 1:
            W.pump()
        DMA("sp" if tt == 0 else "pool", X[:, :, tt * 512:(tt + 1) * 512], xTv[:, :, tt * 512:(tt + 1) * 512],
            "x%d" % tt, (), [XT[k][tt] for k in range(8)])

    MSET(EPSC[:, :], EPS, [T_EPSC])
    MSET(ONESB[:, :], 1.0 / 1024.0, [T_ONESB])
    MSET(ONEROWB[:, :], 1.0, [T_ONEROWB])
    MSET(ONEROWF, 1.0, [T_ONEROWF])
    MSET(ONECOLF, 1.0, [T_ONECOLF])
    CP(IDB[:, :], IDENT, [T_IDENT], [T_IDB])
    CP(CWB[:, :], VECS[:, C_CW:C_CW + 248], [T_VECS], [T_CWB])
    CP(BVROW[:, :], ROWS[0:1, 0:1024], [T_ROWS], [T_BVROW])
    ACT(CACT[:, :], CVEC[:, :], AF.Silu, [T_CVEC], [T_CACT])
    TT(WSM.rearrange("p (h t) -> p h t", h=8), WST.rearrange("p (h t) -> p h t", h=8),
       MASK.unsqueeze(1).broadcast_to([128, 8, 128]), ALU.mult, [T_WST, T_MASK], [T_WSM])
    CP(WSB[:, :, :], WSM.rearrange("p (h t) -> p h t", h=8), [T_WSM], [T_WSB])
    for g in range(2):
        b = PB.take()
        MM(PS[b][0:1, :], ONECOLF[:, 0:1], WSM[:, g * 512:(g + 1) * 512], True, True,
           [T_ONECOLF, T_WSM], [PT[b]])
        CP(ROWSUM[0:1, g * 512:(g + 1) * 512], PS[b][0:1, :], [PT[b]], [T_ROWSUM])
    CHf = CH[:, :, :].rearrange("p h t -> p (h t)")
    for g in range(2):
        b = PB.take()
        for hq in range(4):
            h = 4 * g + hq
            o_ = PS[b][:, hq * 128:(hq + 1) * 128]
            MM(o_, ROWS[0:1, 2048 + h * 128:2048 + (h + 1) * 128], ROWSUM[0:1, h * 128:(h + 1) * 128],
               True, False, [T_ROWS, T_ROWSUM], [PT[b]])
            MM(o_, ONEROWF[0:1, 0:128], ROWS[0:1, 1024 + h * 128:1024 + (h + 1) * 128],
               False, True, [T_ROWS, T_ONEROWF], [PT[b]])
        CP(CHf[:, g * 512:(g + 1) * 512], PS[b][:, :], [PT[b]], [T_CH])

    def ada_phase(g, blks=range(6)):
        blks = list(blks)
        b = PB.take(hold=True)
        for blk in blks:
            i, s = W.acquire("ada%d_%d" % (g, blk))
            for cc in range(4):
                jl = blk * 4 + cc
                for k in range(8):
                    MM(PS[b][:, jl:jl + 1], wcol(s, k, cc * 128), CACT[:, k:k + 1], k == 0, k == 7,
                       [RTile[s], T_CACT], [PT[b]])
            W.release(i)
        c0, c1 = 4 * blks[0], 4 * blks[-1] + 4
        first = blks[0] == 0
        mt = MODT[g] if first else MODT[3]
        dt_ = DERT[g] if first else DERT[3]
        TT(MOD[:, 24 * g + c0:24 * g + c1], PS[b][:, c0:c1],
           VECS[:, C_ADAB + 24 * g + c0:C_ADAB + 24 * g + c1], ALU.add, [PT[b], T_VECS], [mt])
        PB.drop(b)
        if first:
            ncol = (C_NF1, C_NMIX, C_NF2)[g]
            acol = (0, 16, 24)[g]
            STT(DER[:, acol:acol + 8], MOD[:, 24 * g + 8:24 * g + 16], 1.0, VECS[:, ncol:ncol + 8],
                ALU.add, ALU.mult, [mt, T_VECS], [dt_])
        if g != 1 and blks[-1] == 5:
            gcol = (8, None, 32)[g]
            TS(DER[:, gcol:gcol + 8], MOD[:, 24 * g + 16:24 * g + 24], 0.5, None, ALU.mult, None,
               [mt], [dt_])

    def stats_tile(tt):
        b = PB.take()
        for k in range(8):
            sq, sqt = SQ.next()
            ACT(sq, Xap(k, tt), AF.Square, [XT[k][tt]], [sqt])
            MM(PS[b][:, :], ONESB[:, :], sq, k == 0, k == 7, [sqt, T_ONESB], [PT[b]])
        r, rt = RT.next()
        RSQRT(r, PS[b][:, :], [PT[b], T_EPSC], rt)
        return r, rt

    def hmap_full(k, tt):
        return Hap(k, tt), HT[k][tt]

    def hmap_half(hh):
        return lambda k, tt: (Hap(k, tt - 2 * hh), HT[k][tt - 2 * hh])

    def apply_tile(tt, r, rt, g, hmap):
        acol = (0, 16, 24)[g]
        bcol = 24 * g
        for k in range(8):
            tm, tmt = TMPN.next()
            TT(tm, Xap(k, tt), r, ALU.mult, [XT[k][tt], rt], [tmt])
            hap, htile = hmap(k, tt)
            ACT(hap, tm, AF.Identity, [tmt, DERT[g], MODT[g]], [htile],
                bias=MOD[:, bcol + k:bcol + k + 1], scale=DER[:, acol + k:acol + k + 1])

    def norm_tile(tt, g, hmap):
        r, rt = stats_tile(tt)
        apply_tile(tt, r, rt, g, hmap)

    def ffn_phase(f, g, pre_dn7=None, dn7_hook=None, pre_dn0=None, gt=None):
        gcol = (8, None, 32)[g]
        gt = DERT[g] if gt is None else gt
        sc = Scr()
        ATb = [sc.take(16384, BF16).rearrange("p (s t) -> p s t", s=4) for _ in range(2)]
        SGb = Buf([sc.take(2048, F32) for _ in range(2)])
        ATT = [[[Tile() for _ in range(4)] for _ in range(4)] for _ in range(2)]
        new_tiles = [t for a in ATT for b_ in a for t in b_] + SGb.tiles
        alias_into(new_tiles, ffn_phase.prev_tiles)
        ffn_phase.prev_tiles = new_tiles

        def GU(j):
            ig, sg_ = W.acquire("%sg%d" % (f, j))
            iu, su = W.acquire("%su%d" % (f, j))
            buf = j % 2
            order = [(hs, tt) for tt in range(4) for hs in range(4)] if j == 0 else \
                    [(hs, tt) for hs in range(4) for tt in range(4)]
            for hs, tt in order:
                bg = PB.take()
                bu = PB.take()
                for k in range(8):
                    MM(PS[bg][:, :], wcol(sg_, k, hs * 128), Hap(k, tt), k == 0, k == 7,
                       [RTile[sg_], HT[k][tt]], [PT[bg]])
                for k in range(8):
                    MM(PS[bu][:, :], wcol(su, k, hs * 128), Hap(k, tt), k == 0, k == 7,
                       [RTile[su], HT[k][tt]], [PT[bu]])
                sgap, sgt = SGb.next()
                ACT(sgap, PS[bg][:, :], AF.Silu, [PT[bg]], [sgt])
                TT(ATb[buf][:, hs, tt * 512:(tt + 1) * 512], PS[bu][:, :], sgap, ALU.mult,
                   [PT[bu], sgt], [ATT[buf][hs][tt]])
            W.release(ig)
            W.release(iu)

        def DN(j, hook=None, pre=None):
            idd, sd = pre if pre is not None else W.acquire("%sd%d" % (f, j))
            buf = j % 2
            for tt in range(4):
                for o in range(8):
                    b = PB.take()
                    for s in range(4):
                        MM(PS[b][:, :], RING[:, sd, s, o // 4, (o % 4) * 128:(o % 4 + 1) * 128],
                           ATb[buf][:, s, tt * 512:(tt + 1) * 512], s == 0, s == 3,
                           [RTile[sd], ATT[buf][s][tt]], [PT[b]])
                    STT(Xap(o, tt), PS[b][:, :], DER[:, gcol + o:gcol + o + 1], Xap(o, tt), ALU.mult, ALU.add,
                        [PT[b], XT[o][tt], gt], [XT[o][tt]])
                if hook is not None:
                    hook(tt)
            W.release(idd)

        GU(0)
        for j in range(1, 8):
            GU(j)
            if j == 1 and pre_dn0 is not None:
                pre_dn0()
            DN(j - 1)
        pre7 = W.acquire("%sd7" % f)
        if pre_dn7 is not None:
            pre_dn7()
        DN(7, dn7_hook, pre7)

    ffn_phase.prev_tiles = setup_tiles

    mix_state = {}

    def mixer_half(hh, first, pre_m7, m7_hooks, after_prologue=None):
        sc = Scr()
        UZ = sc.take(8 * 1056 * 2, BF16).rearrange("p (o t) -> p o t", o=8)
        SGb = Buf([sc.take(2048, F32) for _ in range(2)])
        wbase = sc.off
        VGb = Buf([sc.take(4096, F32) for _ in range(2)])
        VNb = Buf([sc.take(2048, BF16) for _ in range(2)])
        TMPV = sc.take(4096, F32).rearrange("p (h t) -> p h t", h=8)
        T_TMPV = Tile()
        sc.off = wbase
        C32 = sc.take(16384, F32).rearrange("p (o t) -> p o t", o=8)
        C32T = [Tile() for _ in range(8)]
        m2_tiles = VGb.tiles + VNb.tiles + [T_TMPV]
        m5_tiles = C32T
        if first:
            UZT = [[Tile() for _ in range(2)] for _ in range(8)]
            ZHT = Tile()
            alias_into([t for a in UZT for t in a] + [ZHT] + SGb.tiles + m2_tiles, ffn_phase.prev_tiles)
            mix_state["UZT"] = UZT
            mix_state["ZHT"] = ZHT
            mix_state["SGt"] = SGb.tiles
            MSET(UZ[:, :, 0:32], 0.0, [ZHT])
        else:
            UZT = mix_state["UZT"]
            ZHT = mix_state["ZHT"]
            SGb.tiles = mix_state["SGt"]
            alias_into(m2_tiles, mix_state["m5"])
        mix_state["m5"] = m5_tiles

        def Uap(o, tl):
            return UZ[:, o, 32 + tl * 512:32 + (tl + 1) * 512]

        def bin_col(part, o):
            c = C_BIN + part * 8 + o
            return VECS[:, c:c + 1]

        i0, s0 = W.acquire("m%din0" % hh)
        i1, s1 = W.acquire("m%din1" % hh)
        i2, s2 = W.acquire("m%din2" % hh)
        i3, s3 = W.acquire("m%din3" % hh)

        def m1_tile(o, tl):
            s = (s0, s1)[o // 4]
            oc = o % 4
            b = PB.take()
            for k in range(8):
                MM(PS[b][:, :], wcol(s, k, oc * 128), Hap(k, tl), k == 0, k == 7,
                   [RTile[s], HT[k][tl]], [PT[b]])
            ACT(Uap(o, tl), PS[b][:, :], AF.Gelu_apprx_tanh, [PT[b], T_VECS], [UZT[o][tl]],
                bias=bin_col(0, o))

        m3w = {}

        def m3_tile(o, tl):
            sgx, sa = m3w[o // 4]
            oc = o % 4
            bA = PB.take()
            bB = PB.take()
            for k in range(8):
                MM(PS[bA][:, :], wcol(sgx, k, oc * 128), Hap(k, tl), k == 0, k == 7,
                   [RTile[sgx], HT[k][tl]], [PT[bA]])
            for k in range(8):
                MM(PS[bB][:, :], wcol(sa, k, oc * 128), Uap(k, tl), k == 0, k == 7,
                   [RTile[sa], UZT[k][tl]], [PT[bB]])
            sgap, sgt = SGb.next()
            ACT(sgap, PS[bA][:, :], AF.Sigmoid, [PT[bA], T_VECS], [sgt], bias=bin_col(4, o))
            TT(Hap(o, 2 + tl), PS[bB][:, :], sgap, ALU.mult, [PT[bB], sgt], [HT[o][2 + tl]])

        cst = {}

        def proj_pe(tc):
            tl = tc // 4
            c0 = tl * 512 + (tc % 4) * 128
            bb = []
            for nh in range(2):
                s = (s2, s3)[nh]
                b = PB.take(hold=True)
                bb.append(b)
                for k in range(8):
                    MM(PS[b][:, :], H[:, k, c0:c0 + 128], RING[:, s, k // 2, k % 2, :], k == 0, False,
                       [HT[k][tl], RTile[s]], [PT[b]])
                MM(PS[b][:, :], ONEROWB[0:1, :], BVROW[0:1, nh * 512:(nh + 1) * 512], False, True,
                   [T_ONEROWB, T_BVROW], [PT[b]])
            cst[tc] = {"bb": bb, "tl": tl, "c0": c0}

        def proj_act(tc):
            d = cst[tc]
            vg, vgt = VGb.next()
            st_ap, st_t = STATSB.next()
            for nh in range(2):
                b = d["bb"][nh]
                ACT(vg[:, nh * 512:(nh + 1) * 512], PS[b][:, :], AF.Gelu_apprx_tanh, [PT[b]], [vgt])
                S.emit("dve", lambda e, o=st_ap[:, nh * 6:(nh + 1) * 6], i_=vg[:, nh * 512:(nh + 1) * 512]:
                       e.bn_stats(out=o, in_=i_), [vgt], [st_t])
                PB.drop(b)
            d.update(vg=vg, vgt=vgt, st_ap=st_ap, st_t=st_t)

        def chain(tc):
            d = cst[tc]
            mv, mvt = MVTB.next()
            S.emit("dve", lambda e, o=mv[:, 0:2], i_=d["st_ap"]: e.bn_aggr(out=o, in_=i_), [d["st_t"]], [mvt])
            RSQRT_DVE(mv[:, 2:3], mv[:, 3:4], mv[:, 1:2], [mvt], mvt)
            vn, vnt = VNb.next()
            TS(vn, d["vg"], mv[:, 0:1], mv[:, 2:3], ALU.subtract, ALU.mult, [d["vgt"], mvt], [vnt])
            d.update(vn=vn, vnt=vnt)

        def spatial(tc):
            d = cst.pop(tc)
            tl, c0, vn, vnt = d["tl"], d["c0"], d["vn"], d["vnt"]
            b2 = [PB.take(), PB.take()]
            for h in range(8):
                MM(PS[b2[h // 4]][:, (h % 4) * 128:(h % 4 + 1) * 128], vn[:, h * 128:(h + 1) * 128], WSB[:, h, :],
                   True, True, [vnt, T_WSB], [PT[b2[h // 4]]])
            for h in range(8):
                STT(TMPV[:, h, :], PS[b2[h // 4]][:, (h % 4) * 128:(h % 4 + 1) * 128],
                    VECS[:, C_SLG + h:C_SLG + h + 1], CH[:, h, :], ALU.mult, ALU.add,
                    [PT[b2[h // 4]], T_VECS, T_CH], [T_TMPV])
            ucols = UZ[:, :, 32 + c0:32 + c0 + 128]
            ut = [UZT[o][tl] for o in range(8)]
            TT(ucols, TMPV, ucols, ALU.mult, [T_TMPV] + ut, ut)

        proj_pe(0)
        proj_act(0)
        proj_pe(1)
        chain(0)
        for o in range(8):
            m1_tile(o, 0)
        if after_prologue is not None:
            after_prologue()
        for tc in range(8):
            if tc + 2 < 8:
                proj_pe(tc + 2)
            if tc > 0:
                chain(tc)
            if tc + 1 < 8:
                proj_act(tc + 1)
            if tc < 4:
                m1_tile(2 * tc, 1)
                m1_tile(2 * tc + 1, 1)
                if tc == 3:
                    W.release(i0)
                    W.release(i1)
            else:
                if tc == 4:
                    m3ids = []
                    for g in range(2):
                        ig, sgx = W.acquire("m%din%d" % (hh, 8 + g))
                        ia, sa = W.acquire("m%dwa%d" % (hh, g))
                        m3w[g] = (sgx, sa)
                        m3ids += [ig, ia]
                m3_tile(2 * (tc - 4), 0)
                m3_tile(2 * (tc - 4) + 1, 0)
            spatial(tc)
        W.release(i2)
        W.release(i3)
        for o in range(8):
            m3_tile(o, 1)
        for i_ in m3ids:
            W.release(i_)

        ctiles = [(1, o) for o in range(8)] + [(0, o) for o in range(8)]
        bmq = {}
        dg = {}
        evs = {}
        chn = {}

        def build_dg(idx):
            st, o = ctiles[idx]
            sl = W.take_raw()
            DG = RING[:, sl].rearrange("p a b n -> p (a b n)")[:, 0:3968].rearrange("p (k j) -> p k j", k=31)
            TT(DG, IDB[:, :].unsqueeze(1).broadcast_to([128, 31, 128]),
               CWB[:, o * 31:(o + 1) * 31].unsqueeze(2).broadcast_to([128, 31, 128]), ALU.mult,
               [T_IDB, T_CWB], [RTile[sl]])
            dg[idx] = (sl, DG)

        def stats_mm(idx):
            st, o = ctiles[idx]
            bm, bq = bmq[st]
            cb, cbt, cq, cqt = evs.pop(idx)
            MM(PS[bm][:, :], ONESB[:, :], cb, o == 0, o == 7, [cbt, T_ONESB], [PT[bm]])
            MM(PS[bq][:, :], ONESB[:, :], cq, o == 0, o == 7, [cqt, T_ONESB], [PT[bq]])

        def stat_chain(st):
            bm, bq = bmq[st]
            mean_sb, mean_t = RT.next()
            rs, rs_t = RT.next()
            tq, tq_t = TMPN.next()
            mr, mr_t = TMPN.next()
            CP(mean_sb, PS[bm][:, :], [PT[bm]], [mean_t])
            TT(tq, mean_sb, mean_sb, ALU.mult, [mean_t], [tq_t])
            TT(tq, PS[bq][:, :], tq, ALU.subtract, [PT[bq], tq_t], [tq_t])
            RSQRT(rs, tq, [tq_t, T_EPSC], rs_t)
            TT(mr, mean_sb, rs, ALU.mult, [mean_t, rs_t], [mr_t])
            PB.drop(bm)
            PB.drop(bq)
            chn[st] = (rs, rs_t, mr, mr_t)
            chn["free%d" % st] = Buf([mean_sb, tq])
            chn["free%d" % st].tiles = [mean_t, tq_t]

        def norm_o(st, o):
            rs, rs_t, mr, mr_t = chn[st]
            TT(C32[:, o, :], C32[:, o, :], rs, ALU.mult, [C32T[o], rs_t], [C32T[o]])
            TT(C32[:, o, :], C32[:, o, :], mr, ALU.subtract, [C32T[o], mr_t], [C32T[o]])
            ACT(Uap(o, st), C32[:, o, :], AF.Silu, [C32T[o], T_VECS], [UZT[o][st]],
                bias=VECS[:, C_CLB + o:C_CLB + o + 1], scale=VECS[:, C_CLG + o:C_CLG + o + 1])

        W.reserve = 2
        for g in range(2):
            iv, sv = W.acquire("m%din%d" % (hh, 4 + g))
            ic, scx = W.acquire("m%din%d" % (hh, 6 + g))
            for oc in range(4):
                o = g * 4 + oc
                for tl in range(2):
                    bA = PB.take()
                    bB = PB.take()
                    for k in range(8):
                        MM(PS[bA][:, :], wcol(sv, k, oc * 128), Hap(k, tl), k == 0, k == 7,
                           [RTile[sv], HT[k][tl]], [PT[bA]])
                    for k in range(8):
                        MM(PS[bB][:, :], wcol(scx, k, oc * 128), Hap(k, tl), k == 0, k == 7,
                           [RTile[scx], HT[k][tl]], [PT[bB]])
                    sgap, sgt = SGb.next()
                    ACT(sgap, PS[bB][:, :], AF.Sigmoid, [PT[bB], T_VECS], [sgt], bias=bin_col(3, o))
                    STT(Uap(o, tl), PS[bA][:, :], bin_col(2, o), sgap, ALU.add, ALU.mult,
                        [PT[bA], sgt, T_VECS], [UZT[o][tl]])
            W.release(iv)
            W.release(ic)
            if g == 0:
                build_dg(0)

        alias_into(m5_tiles, m2_tiles)
        if hh == 0:
            CP(HALO[:, :, 2:32], UZ[:, :, 32 + 994:32 + 1024], [UZT[o][1] for o in range(8)], [T_HALO])
        else:
            CP(UZ[:, :, 2:32], HALO[:, :, 2:32], [T_HALO], [ZHT])
        cbank = {}
        for idx in range(18):
            if idx < 16:
                st, o = ctiles[idx]
                if o == 0:
                    bmq[st] = (PB.take(hold=True), PB.take(hold=True))
                sl, DG = dg.pop(idx)
                b = PB.take(hold=True)
                cbank[idx] = b
                rds = [RTile[sl], UZT[o][st], (UZT[o][st - 1] if st > 0 else ZHT)]
                for k in range(31):
                    cc = 2 + st * 512 + k
                    MM(PS[b][:, :], DG[:, k, :], UZ[:, o, cc:cc + 512], k == 0, k == 30, rds, [PT[b]])
                W.give_raw(sl)
                if idx + 1 < 16:
                    build_dg(idx + 1)
            if 0 <= idx - 2 < 16:
                stats_mm(idx - 2)
                if ctiles[idx - 2] == (1, 7):
                    stat_chain(1)
            if 1 <= idx <= 16:
                j = idx - 1
                st, o = ctiles[j]
                b = cbank.pop(j)
                if st == 0:
                    norm_o(1, o)
                cvb = VECS[:, C_CVB + o:C_CVB + o + 1]
                ACT(C32[:, o, :], PS[b][:, :], AF.Identity, [PT[b], T_VECS], [C32T[o]], bias=cvb)
                cb, cbt = SQ.next()
                cq, cqt = SQ.next()
                ACT(cb, PS[b][:, :], AF.Identity, [PT[b], T_VECS], [cbt], bias=cvb)
                ACT(cq, PS[b][:, :], AF.Square, [PT[b], T_VECS], [cqt], bias=cvb)
                PB.drop(b)
                evs[j] = (cb, cbt, cq, cqt)
        stat_chain(0)
        for o in range(8):
            norm_o(0, o)
        W.reserve = 0
        W.pump()

        m6 = []
        for g in range(2):
            ig, sgx = W.acquire("m%din%d" % (hh, 10 + g))
            ib, sbx = W.acquire("m%dwb%d" % (hh, g))
            m6.append((ig, sgx, ib, sbx))
        for tl in (1, 0):
            for g in range(2):
                ig, sgx, ib, sbx = m6[g]
                for oc in range(4):
                    o = g * 4 + oc
                    bA = PB.take()
                    bB = PB.take()
                    for k in range(8):
                        MM(PS[bA][:, :], wcol(sgx, k, oc * 128), Hap(k, tl), k == 0, k == 7,
                           [RTile[sgx], HT[k][tl]], [PT[bA]])
                    for k in range(8):
                        MM(PS[bB][:, :], wcol(sbx, k, oc * 128), Uap(k, tl), k == 0, k == 7,
                           [RTile[sbx], UZT[k][tl]], [PT[bB]])
                    sgap, sgt = SGb.next()
                    ACT(sgap, PS[bA][:, :], AF.Sigmoid, [PT[bA], T_VECS], [sgt], bias=bin_col(5, o))
                    tm, tmt = TMPN.next()
                    TT(tm, PS[bB][:, :], sgap, ALU.mult, [PT[bB], sgt], [tmt])
                    TT(Hap(o, 2 + tl), Hap(o, 2 + tl), tm, ALU.add, [HT[o][2 + tl], tmt], [HT[o][2 + tl]])
        for ig, sgx, ib, sbx in m6:
            W.release(ig)
            W.release(ib)

        if pre_m7 is not None:
            pre_m7()

        ti = 0
        for g in range(2):
            io, so = W.acquire("m%dwo%d" % (hh, g))
            for oc in range(4):
                o = g * 4 + oc
                for tl in range(2):
                    if ti in m7_hooks:
                        m7_hooks[ti]()
                    ti += 1
                    b = PB.take()
                    for k in range(8):
                        MM(PS[b][:, :], wcol(so, k, oc * 128), Hap(k, 2 + tl), k == 0, k == 7,
                           [RTile[so], HT[k][2 + tl]], [PT[b]])
                    tt = 2 * hh + tl
                    STT(Xap(o, tt), PS[b][:, :], MOD[:, 40 + o:41 + o], Xap(o, tt), ALU.mult, ALU.add,
                        [PT[b], XT[o][tt], MODT[1]], [XT[o][tt]])
            W.release(io)
        return [t for a in UZT for t in a] + [ZHT] + SGb.tiles + m2_tiles + m5_tiles

    outv = d_out.rearrange("(k p) t -> p k t", p=128)
    T_OUT = Tile()

    def final_tile(tt):
        r, rt = stats_tile(tt)
        for k in range(8):
            STT(Xap(k, tt), Xap(k, tt), VECS[:, C_NFIN + k:C_NFIN + k + 1], r, ALU.mult, ALU.mult,
                [XT[k][tt], rt, T_VECS], [XT[k][tt]])
        DMA("sp", outv[:, :, tt * 512:(tt + 1) * 512], X[:, :, tt * 512:(tt + 1) * 512], "out",
            [XT[k][tt] for k in range(8)], [T_OUT])

    r0 = stats_tile(0)
    ada_phase(0, range(4))
    apply_tile(0, r0[0], r0[1], 0, hmap_full)
    norm_tile(1, 0, hmap_full)
    norm_tile(2, 0, hmap_full)
    norm_tile(3, 0, hmap_full)

    h1 = {}

    def hook1(tt):
        if tt == 1:
            norm_tile(0, 1, hmap_half(0))
        elif tt == 2:
            h1[1] = stats_tile(1)

    def after_pro0():
        apply_tile(1, h1[1][0], h1[1][1], 1, hmap_half(0))

    ffn_phase("f1", 0, pre_dn7=lambda: ada_phase(1), dn7_hook=hook1,
              pre_dn0=lambda: ada_phase(0, (4, 5)), gt=DERT[3])

    hst = {}

    def mk_hooks(tts, g, hmap):
        def st_(tt):
            hst[tt] = stats_tile(tt)

        def ap_(tt):
            r, rt = hst.pop(tt)
            apply_tile(tt, r, rt, g, hmap)
        return {0: lambda: st_(tts[0]), 4: lambda: ap_(tts[0]), 8: lambda: st_(tts[1]), 12: lambda: ap_(tts[1])}

    mixer_half(0, True, None, mk_hooks((2, 3), 1, hmap_half(1)), after_pro0)
    mtiles = mixer_half(1, False, lambda: ada_phase(2), mk_hooks((0, 1), 2, hmap_full))
    ffn_phase.prev_tiles = mtiles
    norm_tile(2, 2, hmap_full)
    norm_tile(3, 2, hmap_full)

    def hook2(tt):
        if tt >= 1:
            final_tile(tt - 1)

    ffn_phase("f2", 2, dn7_hook=hook2)
    final_tile(3)
    T_OUT.w = {"out": S.dsem["out"]}
    S.emit("sp", None, [T_OUT], ())

    assert W.cur == len(blocks) and W.nxt == len(blocks), (W.cur, W.nxt, len(blocks))

    sig = {e: set() for e in ("pe", "act", "dve", "pool")}
    for e in S.ENG:
        for fn, waits, tok in S.q[e]:
            for k, v in waits:
                if k in sig:
                    sig[k].add(v)
    rank = {}
    for e, st_ in sig.items():
        rank[e] = {v: i + 1 for i, v in enumerate(sorted(st_))}

    def replay(ename, handle):
        for fn, waits, tok in S.q[ename]:
            for k, v in waits:
                if k in sem_eng:
                    handle.wait_ge(sem_eng[k], rank[k][v])
                else:
                    handle.wait_ge(sem_dma[k], v)
            if fn is None:
                continue
            inst = fn(handle)
            if tok[0] in sem_dma:
                inst.then_inc(sem_dma[tok[0]], 16)
            elif tok[1] in sig[tok[0]]:
                inst.then_inc(sem_eng[tok[0]], 1)

    with nc.Block() as block:
        @block.sync
        def _(e):
            replay("sp", e)

        @block.gpsimd
        def _(e):
            replay("pool", e)

        @block.scalar
        def _(e):
            replay("act", e)

        @block.vector
        def _(e):
            replay("dve", e)

        @block.tensor
        def _(e):
            replay("pe", e)

    es.close()
    return nc


def prep_inputs(inp):
    g = lambda n: np.asarray(inp[n], dtype=np.float32)
    x = g("x")
    c = g("c")
    vecs = np.zeros((128, NV), np.float32)

    def put(col, v):
        v = np.asarray(v, np.float32).reshape(-1, 128).T
        vecs[:, col:col + v.shape[1]] = v

    put(C_NF1, g("norm_ffn1")[0])
    put(C_NMIX, g("norm_mix")[0])
    put(C_NF2, g("norm_ffn2")[0])
    put(C_NFIN, g("norm_final"))
    put(C_BIN, g("mix_b_in")[0])
    put(C_SLG, g("sgu_ln_g")[0])
    put(C_SLB, g("sgu_ln_b")[0])
    put(C_CVB, g("conv_b")[0])
    put(C_CLG, g("conv_ln_g")[0])
    put(C_CLB, g("conv_ln_b")[0])
    put(C_ADAB, g("ada_b")[0])
    cw = g("conv_w")[0]
    for o in range(8):
        vecs[:, C_CW + o * 31:C_CW + (o + 1) * 31] = cw[:, o * 128:(o + 1) * 128].T
    wsT = np.ascontiguousarray(g("sgu_w_s")[0].transpose(2, 0, 1))
    s_idx = np.arange(128)
    mask = (s_idx[:, None] <= s_idx[None, :]).astype(np.float32)
    ident = np.eye(128, dtype=np.float32)
    rows = np.concatenate([g("mix_b_in")[0][1024:2048], g("sgu_b_s")[0].reshape(-1), g("sgu_ln_b")[0]])[None, :]
    rows = np.ascontiguousarray(rows, dtype=np.float32)
    shared = {
        "vecs": vecs, "wsT": wsT, "mask": mask, "ident": ident, "rows": rows,
        "ada_w": np.ascontiguousarray(g("ada_w")[0]),
        "f1g": np.ascontiguousarray(g("ffn1_w_gate")[0]), "f1u": np.ascontiguousarray(g("ffn1_w_up")[0]),
        "f1d": np.ascontiguousarray(g("ffn1_w_down")[0]),
        "f2g": np.ascontiguousarray(g("ffn2_w_gate")[0]), "f2u": np.ascontiguousarray(g("ffn2_w_up")[0]),
        "f2d": np.ascontiguousarray(g("ffn2_w_down")[0]),
        "w_in": np.ascontiguousarray(g("mix_w_in")[0]),
        "w_a": np.ascontiguousarray(g("w_branch_a")[0]), "w_b": np.ascontiguousarray(g("w_branch_b")[0]),
        "w_o": np.ascontiguousarray(g("w_out")[0]),
    }
    in_maps = []
    for b in range(8):
        m = dict(shared)
        m["xT"] = np.ascontiguousarray(x[b].T)
        m["cvec"] = np.ascontiguousarray(c[b].reshape(8, 128).T)
        in_maps.append(m)
    return in_maps


def kernel(**inputs):
    in_maps = prep_inputs(inputs)
    nc = build_program(STAGE if STAGE < 9 else 4)
    res = run_bass_kernel_spmd(nc, in_maps, core_ids=list(range(8)))
    out = np.stack([np.asarray(r["outT"], dtype=np.float32).T for r in res.results], axis=0)
    return np.ascontiguousarray(out)
```

```python
import os
from contextlib import ExitStack

import numpy as np
import concourse.bass as bass
import concourse.mybir as mybir
from concourse.bass_utils import run_bass_kernel_spmd

F32 = mybir.dt.float32
BF16 = mybir.dt.bfloat16
AF = mybir.ActivationFunctionType
ALU = mybir.AluOpType

EPS = 1e-6
NS = 6
NV = 440
STAGE = int(os.environ.get("KSTAGE", "9"))

C_NF1, C_NMIX, C_NF2, C_NFIN = 0, 8, 16, 24
C_BIN = 32
C_SLG, C_SLB = 80, 88
C_CVB, C_CLG, C_CLB = 96, 104, 112
C_ADAB = 120
C_CW = 192


class Tile:
    __slots__ = ("w", "r")

    def __init__(self):
        self.w = {}
        self.r = {}


def alias_into(new_tiles, old_tiles):
    w, r = {}, {}
    for t in old_tiles:
        for k, v in t.w.items():
            if w.get(k, 0) < v:
                w[k] = v
        for k, v in t.r.items():
            if r.get(k, 0) < v:
                r[k] = v
    for t in new_tiles:
        t.w = dict(w)
        t.r = dict(r)


class Sched:
    ENG = ("pe", "act", "dve", "pool", "sp")

    def __init__(self):
        self.q = {e: [] for e in self.ENG}
        self.seq = {e: 0 for e in self.ENG}
        self.known = {e: {} for e in self.ENG}
        self.dsem = {}

    def emit(self, eng, fn, reads=(), writes=(), dma=None):
        raw, oth = {}, {}
        for t in reads:
            for k, v in t.w.items():
                if raw.get(k, 0) < v:
                    raw[k] = v
        for t in writes:
            for k, v in t.w.items():
                if oth.get(k, 0) < v:
                    oth[k] = v
            for k, v in t.r.items():
                if oth.get(k, 0) < v:
                    oth[k] = v
        waits = []
        kn = self.known[eng]
        for k, v in raw.items():
            if k == eng and eng == "pe":
                continue
            if kn.get(k, 0) >= v:
                continue
            kn[k] = v
            waits.append((k, v))
        for k, v in oth.items():
            if k == eng:
                continue
            if kn.get(k, 0) >= v:
                continue
            kn[k] = v
            waits.append((k, v))
        if dma is None:
            self.seq[eng] += 1
            tok = (eng, self.seq[eng])
        else:
            self.dsem[dma] = self.dsem.get(dma, 0) + 16
            tok = (dma, self.dsem[dma])
        self.q[eng].append((fn, waits, tok))
        for t in reads:
            if t.r.get(tok[0], 0) < tok[1]:
                t.r[tok[0]] = tok[1]
        for t in writes:
            t.w = {tok[0]: tok[1]}
            t.r = {}
        return tok


class Banks:
    def __init__(self):
        self.next = 0
        self.held = set()

    def take(self, hold=False):
        for _ in range(16):
            b = self.next
            self.next = (self.next + 1) % 8
            if b not in self.held:
                if hold:
                    self.held.add(b)
                return b
        raise RuntimeError("no psum bank")

    def drop(self, b):
        self.held.discard(b)


class Buf:
    def __init__(self, aps):
        self.aps = aps
        self.tiles = [Tile() for _ in aps]
        self.i = 0

    def next(self):
        j = self.i % len(self.aps)
        self.i += 1
        return self.aps[j], self.tiles[j]


def build_program(stage=STAGE):
    nc = bass.Bass("TRN2", target_bir_lowering=False)
    S = Sched()
    PB = Banks()
    es = ExitStack()

    def din(name, shape):
        return nc.dram_tensor(name, list(shape), F32, kind="ExternalInput").ap()

    d_xT = din("xT", [1024, 2048])
    d_cvec = din("cvec", [128, 8])
    d_vecs = din("vecs", [128, NV])
    d_wsT = din("wsT", [128, 8, 128])
    d_mask = din("mask", [128, 128])
    d_ident = din("ident", [128, 128])
    d_rows = din("rows", [1, 3072])
    d_ada = din("ada_w", [1024, 9216])
    d_f = {}
    for f in ("f1", "f2"):
        d_f[f + "g"] = din(f + "g", [1024, 4096])
        d_f[f + "u"] = din(f + "u", [1024, 4096])
        d_f[f + "d"] = din(f + "d", [4096, 1024])
    d_win = din("w_in", [1024, 6144])
    d_wa = din("w_a", [1024, 1024])
    d_wb = din("w_b", [1024, 1024])
    d_wo = din("w_o", [1024, 1024])
    d_out = nc.dram_tensor("outT", [1024, 2048], F32, kind="ExternalOutput").ap()

    def sb(name, shape, dt):
        return es.enter_context(nc.sbuf_tensor(name, list(shape), dt))

    X = sb("X", [128, 8, 2048], F32)
    H = sb("H", [128, 8, 2048], BF16)
    RING = sb("RING", [128, NS, 4, 2, 512], BF16)
    VECS = sb("VECS", [128, NV], F32)
    MOD = sb("MOD", [128, 72], F32)
    DER = sb("DER", [128, 40], F32)
    CVEC = sb("CVEC", [128, 8], F32)
    CACT = sb("CACT", [128, 8], BF16)
    ONESB = sb("ONESB", [128, 128], BF16)
    IDB = sb("IDB", [128, 128], BF16)
    WSB = sb("WSB", [128, 8, 128], BF16)
    CH = sb("CH", [128, 8, 128], F32)
    BVROW = sb("BVROW", [1, 1024], BF16)
    ONEROWB = sb("ONEROWB", [1, 128], BF16)
    CWB = sb("CWB", [128, 248], BF16)
    HALO = sb("HALO", [128, 8, 32], BF16)
    RTb = sb("RTb", [128, 2, 512], F32)
    SQb = sb("SQb", [128, 4, 512], BF16)
    TMPNb = sb("TMPNb", [128, 2, 512], F32)
    STATS = sb("STATS", [128, 2, 12], F32)
    MVT = sb("MVT", [128, 2, 4], F32)
    EPSC = sb("EPSC", [128, 8], F32)
    SCRN = 18688
    SCR = sb("SCR", [128, SCRN], BF16)

    PS = [es.enter_context(nc.psum_tensor("ps%d" % i, [128, 512], F32)) for i in range(8)]
    PT = [Tile() for _ in range(8)]

    sem_eng = {e: es.enter_context(nc.semaphore("s_" + e)) for e in ("pe", "act", "dve", "pool")}
    dma_keys = ["c", "out"] + ["x%d" % i for i in range(4)] + ["w%d" % i for i in range(NS)]
    sem_dma = {k: es.enter_context(nc.semaphore("d_" + k)) for k in dma_keys}

    def MM(out, lhsT, rhs, start, stop, reads, writes):
        S.emit("pe", lambda e, o=out, l=lhsT, r=rhs, a=start, b=stop:
               e.matmul(o, lhsT=l, rhs=r, start=a, stop=b), reads, writes)

    def ACT(out, in_, func, reads, writes, bias=None, scale=None):
        kw = {}
        if bias is not None:
            kw["bias"] = bias
        if scale is not None:
            kw["scale"] = scale
        S.emit("act", lambda e, o=out, i=in_, f=func, kw=kw:
               e.activation(out=o, in_=i, func=f, **kw), reads, writes)

    def TT(out, in0, in1, op, reads, writes, eng="dve"):
        S.emit(eng, lambda e, o=out, a=in0, b=in1, p=op:
               e.tensor_tensor(out=o, in0=a, in1=b, op=p), reads, writes)

    def TS(out, in0, s1, s2, op0, op1, reads, writes, eng="dve"):
        if op1 is None:
            S.emit(eng, lambda e, o=out, a=in0, x=s1, p=op0:
                   e.tensor_scalar(out=o, in0=a, scalar1=x, scalar2=None, op0=p), reads, writes)
        else:
            S.emit(eng, lambda e, o=out, a=in0, x=s1, y=s2, p=op0, q=op1:
                   e.tensor_scalar(out=o, in0=a, scalar1=x, scalar2=y, op0=p, op1=q), reads, writes)

    def STT(out, in0, scalar, in1, op0, op1, reads, writes):
        S.emit("dve", lambda e, o=out, a=in0, s=scalar, b=in1, p=op0, q=op1:
               e.scalar_tensor_tensor(out=o, in0=a, scalar=s, in1=b, op0=p, op1=q), reads, writes)

    def RSQRT(out, in_, reads, wtile):
        ACT(out, in_, AF.Sqrt, reads, [wtile], bias=EPSC[:in_.shape[0], 0:1])
        S.emit("dve", lambda e, o=out: e.reciprocal(out=o, in_=o), [wtile], [wtile])

    I32 = mybir.dt.int32

    def RSQRT_DVE(y, a, var, reads, wtile):
        TS(a, var, EPS, None, ALU.add, None, reads, [wtile])
        TS(y.bitcast(I32), a.bitcast(I32), -0.5, 1597463007.0, ALU.mult, ALU.add, [wtile], [wtile])
        for _ in range(3):
            TS(var, y, y, a, ALU.mult, ALU.mult, [wtile], [wtile])
            TS(var, var, -0.5, 1.5, ALU.mult, ALU.add, [wtile], [wtile])
            TS(y, y, var, None, ALU.mult, None, [wtile], [wtile])

    def CP(out, in_, reads, writes, eng="dve"):
        S.emit(eng, lambda e, o=out, i=in_: e.tensor_copy(out=o, in_=i), reads, writes)

    def MSET(ap, val, writes, eng="dve"):
        S.emit(eng, lambda e, a=ap, v=val: e.memset(a, v), (), writes)

    def DMA(q, out, in_, key, reads, writes):
        S.emit(q, lambda e, o=out, i=in_: e.dma_start(out=o, in_=i), reads, writes, dma=key)

    XT = [[Tile() for _ in range(4)] for _ in range(8)]
    HT = [[Tile() for _ in range(4)] for _ in range(8)]
    RTile = [Tile() for _ in range(NS)]
    T_VECS, T_CVEC, T_CACT, T_ONESB, T_IDB, T_WSB, T_CH = (Tile() for _ in range(7))
    T_BVROW, T_ONEROWB, T_CWB, T_HALO, T_EPSC = (Tile() for _ in range(5))
    MODT = [Tile() for _ in range(4)]
    DERT = [Tile() for _ in range(4)]
    RT = Buf([RTb[:, i, :] for i in range(2)])
    SQ = Buf([SQb[:, i, :] for i in range(4)])
    TMPN = Buf([TMPNb[:, i, :] for i in range(2)])
    STATSB = Buf([STATS[:, i, :] for i in range(2)])
    MVTB = Buf([MVT[:, i, :] for i in range(2)])

    def Xap(k, tt):
        return X[:, k, tt * 512:(tt + 1) * 512]

    def Hap(k, tt):
        return H[:, k, tt * 512:(tt + 1) * 512]

    class Scr:
        def __init__(self):
            self.off = 0

        def take(self, nbytes, dt, parts=None):
            assert self.off % 32 == 0
            a = self.off // 2
            n2 = nbytes // 2
            assert a + n2 <= SCRN, (a, n2)
            v = SCR[:, a:a + n2] if parts is None else SCR[0:parts, a:a + n2]
            self.off += (nbytes + 31) // 32 * 32
            if dt is F32:
                v = v.bitcast(F32)
            return v

    def colblock(Wd_, c0):
        return Wd_.rearrange("(a b p) n -> p a b n", a=4, b=2, p=128)[:, :, :, c0:c0 + 512]

    def rowblock(Wd_, j):
        return Wd_.rearrange("(s p) (c n) -> p s c n", p=128, c=2)[:, 4 * j:4 * j + 4]

    blocks = []

    def add_ada(g, blks=range(6)):
        for blk in blks:
            blocks.append(("ada%d_%d" % (g, blk), colblock(d_ada, 3072 * g + 512 * blk)))

    def add_ffn(f, before_last=None, before_first_dn=None):
        def gu(j):
            blocks.append(("%sg%d" % (f, j), colblock(d_f[f + "g"], 512 * j)))
            blocks.append(("%su%d" % (f, j), colblock(d_f[f + "u"], 512 * j)))

        def dn(j):
            blocks.append(("%sd%d" % (f, j), rowblock(d_f[f + "d"], j)))
        gu(0)
        for j in range(1, 8):
            gu(j)
            if j == 1 and before_first_dn is not None:
                before_first_dn()
            dn(j - 1)
        dn(7)
        if before_last is not None:
            before_last()

    def add_mixer(hh, before_wo=None):
        def inb(i):
            blocks.append(("m%din%d" % (hh, i), colblock(d_win, 512 * i)))
        for i in (0, 1, 2, 3):
            inb(i)
        for g in range(2):
            inb(8 + g)
            blocks.append(("m%dwa%d" % (hh, g), colblock(d_wa, 512 * g)))
        for g in range(2):
            inb(4 + g)
            inb(6 + g)
        for g in range(2):
            inb(10 + g)
            blocks.append(("m%dwb%d" % (hh, g), colblock(d_wb, 512 * g)))
        if before_wo is not None:
            before_wo()
        for g in range(2):
            blocks.append(("m%dwo%d" % (hh, g), colblock(d_wo, 512 * g)))

    add_ada(0, range(4))
    add_ffn("f1", before_last=lambda: add_ada(1), before_first_dn=lambda: add_ada(0, (4, 5)))
    add_mixer(0)
    add_mixer(1, before_wo=lambda: add_ada(2))
    add_ffn("f2")

    class WS:
        def __init__(self):
            self.nxt = 0
            self.cur = 0
            self.free = list(range(NS))
            self.slot_of = {}
            self.reserve = 0

        def pump(self):
            while self.nxt < len(blocks) and len(self.free) > self.reserve:
                s = self.free.pop(0)
                i = self.nxt
                self.nxt += 1
                DMA("pool", RING[:, s], blocks[i][1], "w%d" % s, (), [RTile[s]])
                self.slot_of[i] = s

        def acquire(self, name):
            i = self.cur
            self.cur += 1
            assert blocks[i][0] == name, (blocks[i][0], name)
            if i not in self.slot_of:
                self.pump()
            assert i in self.slot_of, ("weight block not issued", name)
            return i, self.slot_of[i]

        def release(self, i):
            s = self.slot_of.pop(i)
            self.free.append(s)
            self.pump()

        def take_raw(self):
            assert self.free, "no free ring slot for raw use"
            return self.free.pop(0)

        def give_raw(self, s):
            self.free.append(s)
            self.pump()

    W = WS()

    def wcol(s, k, c0):
        return RING[:, s, k // 2, k % 2, c0:c0 + 128]

    scr = Scr()
    WST = scr.take(4096, F32)
    WSM = scr.take(4096, F32)
    MASK = scr.take(512, F32)
    IDENT = scr.take(512, F32)
    ONECOLF = scr.take(32, F32)
    ROWS = scr.take(12288, F32, parts=1)
    ROWSUM = scr.take(4096, F32, parts=1)
    ONEROWF = scr.take(512, F32, parts=1)
    T_WST, T_WSM, T_MASK, T_IDENT, T_ONECOLF, T_ROWS, T_ROWSUM, T_ONEROWF = (Tile() for _ in range(8))
    setup_tiles = [T_WST, T_WSM, T_MASK, T_IDENT, T_ONECOLF, T_ROWS, T_ROWSUM, T_ONEROWF]

    DMA("sp", VECS[:, :], d_vecs, "c", (), [T_VECS])
    DMA("sp", CVEC[:, :], d_cvec, "c", (), [T_CVEC])
    DMA("sp", WST.rearrange("p (h t) -> p h t", h=8), d_wsT, "c", (), [T_WST])
    DMA("sp", MASK, d_mask, "c", (), [T_MASK])
    DMA("sp", IDENT, d_ident, "c", (), [T_IDENT])
    DMA("sp", ROWS, d_rows, "c", (), [T_ROWS])
    for t in (T_VECS, T_CVEC, T_WST, T_MASK, T_IDENT, T_ROWS):
        t.w = {"c": S.dsem["c"]}
    xTv = d_xT.rearrange("(k p) t -> p k t", p=128)
    for tt in range(4):
        if tt == 1:
            W.reserve = NS - 4
            W.pump()
            W.reserve = 0
        DMA("sp" if tt == 0 else "pool", X[:, :, tt * 512:(tt + 1) * 512], xTv[:, :, tt * 512:(tt + 1) * 512],
            "x%d" % tt, (), [XT[k][tt] for k in range(8)])
    W.pump()

    MSET(EPSC[:, :], EPS, [T_EPSC])
    MSET(ONESB[:, :], 1.0 / 1024.0, [T_ONESB])
    MSET(ONEROWB[:, :], 1.0, [T_ONEROWB])
    MSET(ONEROWF, 1.0, [T_ONEROWF])
    MSET(ONECOLF, 1.0, [T_ONECOLF])
    CP(IDB[:, :], IDENT, [T_IDENT], [T_IDB])
    CP(CWB[:, :], VECS[:, C_CW:C_CW + 248], [T_VECS], [T_CWB])
    CP(BVROW[:, :], ROWS[0:1, 0:1024], [T_ROWS], [T_BVROW])
    ACT(CACT[:, :], CVEC[:, :], AF.Silu, [T_CVEC], [T_CACT])
    TT(WSM.rearrange("p (h t) -> p h t", h=8), WST.rearrange("p (h t) -> p h t", h=8),
       MASK.unsqueeze(1).broadcast_to([128, 8, 128]), ALU.mult, [T_WST, T_MASK], [T_WSM])
    CP(WSB[:, :, :], WSM.rearrange("p (h t) -> p h t", h=8), [T_WSM], [T_WSB])
    for g in range(2):
        b = PB.take()
        MM(PS[b][0:1, :], ONECOLF[:, 0:1], WSM[:, g * 512:(g + 1) * 512], True, True,
           [T_ONECOLF, T_WSM], [PT[b]])
        CP(ROWSUM[0:1, g * 512:(g + 1) * 512], PS[b][0:1, :], [PT[b]], [T_ROWSUM])
    CHf = CH[:, :, :].rearrange("p h t -> p (h t)")
    for g in range(2):
        b = PB.take()
        for hq in range(4):
            h = 4 * g + hq
            o_ = PS[b][:, hq * 128:(hq + 1) * 128]
            MM(o_, ROWS[0:1, 2048 + h * 128:2048 + (h + 1) * 128], ROWSUM[0:1, h * 128:(h + 1) * 128],
               True, False, [T_ROWS, T_ROWSUM], [PT[b]])
            MM(o_, ONEROWF[0:1, 0:128], ROWS[0:1, 1024 + h * 128:1024 + (h + 1) * 128],
               False, True, [T_ROWS, T_ONEROWF], [PT[b]])
        CP(CHf[:, g * 512:(g + 1) * 512], PS[b][:, :], [PT[b]], [T_CH])

    def ada_phase(g, blks=range(6)):
        blks = list(blks)
        b = PB.take(hold=True)
        for blk in blks:
            i, s = W.acquire("ada%d_%d" % (g, blk))
            for cc in range(4):
                jl = blk * 4 + cc
                for k in range(8):
                    MM(PS[b][:, jl:jl + 1], wcol(s, k, cc * 128), CACT[:, k:k + 1], k == 0, k == 7,
                       [RTile[s], T_CACT], [PT[b]])
            W.release(i)
        c0, c1 = 4 * blks[0], 4 * blks[-1] + 4
        first = blks[0] == 0
        mt = MODT[g] if first else MODT[3]
        dt_ = DERT[g] if first else DERT[3]
        TT(MOD[:, 24 * g + c0:24 * g + c1], PS[b][:, c0:c1],
           VECS[:, C_ADAB + 24 * g + c0:C_ADAB + 24 * g + c1], ALU.add, [PT[b], T_VECS], [mt])
        PB.drop(b)
        if first:
            ncol = (C_NF1, C_NMIX, C_NF2)[g]
            acol = (0, 16, 24)[g]
            STT(DER[:, acol:acol + 8], MOD[:, 24 * g + 8:24 * g + 16], 1.0, VECS[:, ncol:ncol + 8],
                ALU.add, ALU.mult, [mt, T_VECS], [dt_])
        if g != 1 and blks[-1] == 5:
            gcol = (8, None, 32)[g]
            TS(DER[:, gcol:gcol + 8], MOD[:, 24 * g + 16:24 * g + 24], 0.5, None, ALU.mult, None,
               [mt], [dt_])

    def stats_tile(tt):
        b = PB.take()
        for k in range(8):
            sq, sqt = SQ.next()
            ACT(sq, Xap(k, tt), AF.Square, [XT[k][tt]], [sqt])
            MM(PS[b][:, :], ONESB[:, :], sq, k == 0, k == 7, [sqt, T_ONESB], [PT[b]])
        r, rt = RT.next()
        RSQRT(r, PS[b][:, :], [PT[b], T_EPSC], rt)
        return r, rt

    def hmap_full(k, tt):
        return Hap(k, tt), HT[k][tt]

    def hmap_half(hh):
        return lambda k, tt: (Hap(k, tt - 2 * hh), HT[k][tt - 2 * hh])

    def apply_tile(tt, r, rt, g, hmap):
        acol = (0, 16, 24)[g]
        bcol = 24 * g
        for k in range(8):
            tm, tmt = TMPN.next()
            TT(tm, Xap(k, tt), r, ALU.mult, [XT[k][tt], rt], [tmt])
            hap, htile = hmap(k, tt)
            ACT(hap, tm, AF.Identity, [tmt, DERT[g], MODT[g]], [htile],
                bias=MOD[:, bcol + k:bcol + k + 1], scale=DER[:, acol + k:acol + k + 1])

    def norm_tile(tt, g, hmap):
        r, rt = stats_tile(tt)
        apply_tile(tt, r, rt, g, hmap)

    def ffn_phase(f, g, pre_dn7=None, dn7_hook=None, pre_dn0=None, gt=None):
        gcol = (8, None, 32)[g]
        gt = DERT[g] if gt is None else gt
        sc = Scr()
        ATb = [sc.take(16384, BF16).rearrange("p (s t) -> p s t", s=4) for _ in range(2)]
        SGb = Buf([sc.take(2048, F32) for _ in range(2)])
        ATT = [[[Tile() for _ in range(4)] for _ in range(4)] for _ in range(2)]
        new_tiles = [t for a in ATT for b_ in a for t in b_] + SGb.tiles
        alias_into(new_tiles, ffn_phase.prev_tiles)
        ffn_phase.prev_tiles = new_tiles

        def GU(j):
            ig, sg_ = W.acquire("%sg%d" % (f, j))
            iu, su = W.acquire("%su%d" % (f, j))
            buf = j % 2
            order = [(hs, tt) for tt in range(4) for hs in range(4)] if j == 0 else \
                    [(hs, tt) for hs in range(4) for tt in range(4)]
            for hs, tt in order:
                bg = PB.take()
                bu = PB.take()
                for k in range(8):
                    MM(PS[bg][:, :], wcol(sg_, k, hs * 128), Hap(k, tt), k == 0, k == 7,
                       [RTile[sg_], HT[k][tt]], [PT[bg]])
                for k in range(8):
                    MM(PS[bu][:, :], wcol(su, k, hs * 128), Hap(k, tt), k == 0, k == 7,
                       [RTile[su], HT[k][tt]], [PT[bu]])
                sgap, sgt = SGb.next()
                ACT(sgap, PS[bg][:, :], AF.Silu, [PT[bg]], [sgt])
                TT(ATb[buf][:, hs, tt * 512:(tt + 1) * 512], PS[bu][:, :], sgap, ALU.mult,
                   [PT[bu], sgt], [ATT[buf][hs][tt]])
            W.release(ig)
            W.release(iu)

        def DN(j, hook=None, pre=None):
            idd, sd = pre if pre is not None else W.acquire("%sd%d" % (f, j))
            buf = j % 2
            for tt in range(4):
                for o in range(8):
                    b = PB.take()
                    for s in range(4):
                        MM(PS[b][:, :], RING[:, sd, s, o // 4, (o % 4) * 128:(o % 4 + 1) * 128],
                           ATb[buf][:, s, tt * 512:(tt + 1) * 512], s == 0, s == 3,
                           [RTile[sd], ATT[buf][s][tt]], [PT[b]])
                    STT(Xap(o, tt), PS[b][:, :], DER[:, gcol + o:gcol + o + 1], Xap(o, tt), ALU.mult, ALU.add,
                        [PT[b], XT[o][tt], gt], [XT[o][tt]])
                if hook is not None:
                    hook(tt)
            W.release(idd)

        GU(0)
        for j in range(1, 8):
            GU(j)
            if j == 1 and pre_dn0 is not None:
                pre_dn0()
            DN(j - 1)
        pre7 = W.acquire("%sd7" % f)
        if pre_dn7 is not None:
            pre_dn7()
        DN(7, dn7_hook, pre7)

    ffn_phase.prev_tiles = setup_tiles

    mix_state = {}

    def mixer_half(hh, first, pre_m7, m7_hooks, after_prologue=None):
        sc = Scr()
        UZ = sc.take(8 * 1056 * 2, BF16).rearrange("p (o t) -> p o t", o=8)
        SGb = Buf([sc.take(2048, F32) for _ in range(2)])
        wbase = sc.off
        VGb = Buf([sc.take(4096, F32) for _ in range(2)])
        VNb = Buf([sc.take(2048, BF16) for _ in range(2)])
        TMPV = sc.take(4096, F32).rearrange("p (h t) -> p h t", h=8)
        T_TMPV = Tile()
        sc.off = wbase
        C32 = sc.take(16384, F32).rearrange("p (o t) -> p o t", o=8)
        C32T = [Tile() for _ in range(8)]
        m2_tiles = VGb.tiles + VNb.tiles + [T_TMPV]
        m5_tiles = C32T
        if first:
            UZT = [[Tile() for _ in range(2)] for _ in range(8)]
            ZHT = Tile()
            alias_into([t for a in UZT for t in a] + [ZHT] + SGb.tiles + m2_tiles, ffn_phase.prev_tiles)
            mix_state["UZT"] = UZT
            mix_state["ZHT"] = ZHT
            mix_state["SGt"] = SGb.tiles
            MSET(UZ[:, :, 0:32], 0.0, [ZHT])
        else:
            UZT = mix_state["UZT"]
            ZHT = mix_state["ZHT"]
            SGb.tiles = mix_state["SGt"]
            alias_into(m2_tiles, mix_state["m5"])
        mix_state["m5"] = m5_tiles

        def Uap(o, tl):
            return UZ[:, o, 32 + tl * 512:32 + (tl + 1) * 512]

        def bin_col(part, o):
            c = C_BIN + part * 8 + o
            return VECS[:, c:c + 1]

        i0, s0 = W.acquire("m%din0" % hh)
        i1, s1 = W.acquire("m%din1" % hh)
        i2, s2 = W.acquire("m%din2" % hh)
        i3, s3 = W.acquire("m%din3" % hh)

        def m1_tile(o, tl):
            s = (s0, s1)[o // 4]
            oc = o % 4
            b = PB.take()
            for k in range(8):
                MM(PS[b][:, :], wcol(s, k, oc * 128), Hap(k, tl), k == 0, k == 7,
                   [RTile[s], HT[k][tl]], [PT[b]])
            ACT(Uap(o, tl), PS[b][:, :], AF.Gelu_apprx_tanh, [PT[b], T_VECS], [UZT[o][tl]],
                bias=bin_col(0, o))

        m3w = {}

        def m3_tile(o, tl):
            sgx, sa = m3w[o // 4]
            oc = o % 4
            bA = PB.take()
            bB = PB.take()
            for k in range(8):
                MM(PS[bA][:, :], wcol(sgx, k, oc * 128), Hap(k, tl), k == 0, k == 7,
                   [RTile[sgx], HT[k][tl]], [PT[bA]])
            for k in range(8):
                MM(PS[bB][:, :], wcol(sa, k, oc * 128), Uap(k, tl), k == 0, k == 7,
                   [RTile[sa], UZT[k][tl]], [PT[bB]])
            sgap, sgt = SGb.next()
            ACT(sgap, PS[bA][:, :], AF.Sigmoid, [PT[bA], T_VECS], [sgt], bias=bin_col(4, o))
            TT(Hap(o, 2 + tl), PS[bB][:, :], sgap, ALU.mult, [PT[bB], sgt], [HT[o][2 + tl]])

        cst = {}

        def proj_pe(tc):
            tl = tc // 4
            c0 = tl * 512 + (tc % 4) * 128
            bb = []
            for nh in range(2):
                s = (s2, s3)[nh]
                b = PB.take(hold=True)
                bb.append(b)
                for k in range(8):
                    MM(PS[b][:, :], H[:, k, c0:c0 + 128], RING[:, s, k // 2, k % 2, :], k == 0, False,
                       [HT[k][tl], RTile[s]], [PT[b]])
                MM(PS[b][:, :], ONEROWB[0:1, :], BVROW[0:1, nh * 512:(nh + 1) * 512], False, True,
                   [T_ONEROWB, T_BVROW], [PT[b]])
            cst[tc] = {"bb": bb, "tl": tl, "c0": c0}

        def proj_act(tc):
            d = cst[tc]
            vg, vgt = VGb.next()
            st_ap, st_t = STATSB.next()
            for nh in range(2):
                b = d["bb"][nh]
                ACT(vg[:, nh * 512:(nh + 1) * 512], PS[b][:, :], AF.Gelu_apprx_tanh, [PT[b]], [vgt])
                S.emit("dve", lambda e, o=st_ap[:, nh * 6:(nh + 1) * 6], i_=vg[:, nh * 512:(nh + 1) * 512]:
                       e.bn_stats(out=o, in_=i_), [vgt], [st_t])
                PB.drop(b)
            d.update(vg=vg, vgt=vgt, st_ap=st_ap, st_t=st_t)

        def chain(tc):
            d = cst[tc]
            mv, mvt = MVTB.next()
            S.emit("dve", lambda e, o=mv[:, 0:2], i_=d["st_ap"]: e.bn_aggr(out=o, in_=i_), [d["st_t"]], [mvt])
            RSQRT_DVE(mv[:, 2:3], mv[:, 3:4], mv[:, 1:2], [mvt], mvt)
            vn, vnt = VNb.next()
            TS(vn, d["vg"], mv[:, 0:1], mv[:, 2:3], ALU.subtract, ALU.mult, [d["vgt"], mvt], [vnt])
            d.update(vn=vn, vnt=vnt)

        def spatial(tc):
            d = cst.pop(tc)
            tl, c0, vn, vnt = d["tl"], d["c0"], d["vn"], d["vnt"]
            b2 = [PB.take(), PB.take()]
            for h in range(8):
                MM(PS[b2[h // 4]][:, (h % 4) * 128:(h % 4 + 1) * 128], vn[:, h * 128:(h + 1) * 128], WSB[:, h, :],
                   True, True, [vnt, T_WSB], [PT[b2[h // 4]]])
            for h in range(8):
                STT(TMPV[:, h, :], PS[b2[h // 4]][:, (h % 4) * 128:(h % 4 + 1) * 128],
                    VECS[:, C_SLG + h:C_SLG + h + 1], CH[:, h, :], ALU.mult, ALU.add,
                    [PT[b2[h // 4]], T_VECS, T_CH], [T_TMPV])
            ucols = UZ[:, :, 32 + c0:32 + c0 + 128]
            ut = [UZT[o][tl] for o in range(8)]
            TT(ucols, TMPV, ucols, ALU.mult, [T_TMPV] + ut, ut)

        proj_pe(0)
        proj_act(0)
        proj_pe(1)
        chain(0)
        for o in range(8):
            m1_tile(o, 0)
        if after_prologue is not None:
            after_prologue()
        for tc in range(8):
            if tc + 2 < 8:
                proj_pe(tc + 2)
            if tc > 0:
                chain(tc)
            if tc + 1 < 8:
                proj_act(tc + 1)
            if tc < 4:
                m1_tile(2 * tc, 1)
                m1_tile(2 * tc + 1, 1)
                if tc == 3:
                    W.release(i0)
                    W.release(i1)
            else:
                if tc == 4:
                    m3ids = []
                    for g in range(2):
                        ig, sgx = W.acquire("m%din%d" % (hh, 8 + g))
                        ia, sa = W.acquire("m%dwa%d" % (hh, g))
                        m3w[g] = (sgx, sa)
                        m3ids += [ig, ia]
                m3_tile(2 * (tc - 4), 0)
                m3_tile(2 * (tc - 4) + 1, 0)
            spatial(tc)
        W.release(i2)
        W.release(i3)
        for o in range(8):
            m3_tile(o, 1)
        for i_ in m3ids:
            W.release(i_)

        ctiles = [(1, o) for o in range(8)] + [(0, o) for o in range(8)]
        bmq = {}
        dg = {}
        evs = {}
        chn = {}

        def build_dg(idx):
            st, o = ctiles[idx]
            sl = W.take_raw()
            DG = RING[:, sl].rearrange("p a b n -> p (a b n)")[:, 0:3968].rearrange("p (k j) -> p k j", k=31)
            TT(DG, IDB[:, :].unsqueeze(1).broadcast_to([128, 31, 128]),
               CWB[:, o * 31:(o + 1) * 31].unsqueeze(2).broadcast_to([128, 31, 128]), ALU.mult,
               [T_IDB, T_CWB], [RTile[sl]])
            dg[idx] = (sl, DG)

        def stats_mm(idx):
            st, o = ctiles[idx]
            bm, bq = bmq[st]
            cb, cbt, cq, cqt = evs.pop(idx)
            MM(PS[bm][:, :], ONESB[:, :], cb, o == 0, o == 7, [cbt, T_ONESB], [PT[bm]])
            MM(PS[bq][:, :], ONESB[:, :], cq, o == 0, o == 7, [cqt, T_ONESB], [PT[bq]])

        def stat_chain(st):
            bm, bq = bmq[st]
            mean_sb, mean_t = RT.next()
            rs, rs_t = RT.next()
            tq, tq_t = TMPN.next()
            mr, mr_t = TMPN.next()
            CP(mean_sb, PS[bm][:, :], [PT[bm]], [mean_t])
            TT(tq, mean_sb, mean_sb, ALU.mult, [mean_t], [tq_t])
            TT(tq, PS[bq][:, :], tq, ALU.subtract, [PT[bq], tq_t], [tq_t])
            RSQRT(rs, tq, [tq_t, T_EPSC], rs_t)
            TT(mr, mean_sb, rs, ALU.mult, [mean_t, rs_t], [mr_t])
            PB.drop(bm)
            PB.drop(bq)
            chn[st] = (rs, rs_t, mr, mr_t)
            chn["free%d" % st] = Buf([mean_sb, tq])
            chn["free%d" % st].tiles = [mean_t, tq_t]

        def norm_o(st, o):
            rs, rs_t, mr, mr_t = chn[st]
            TT(C32[:, o, :], C32[:, o, :], rs, ALU.mult, [C32T[o], rs_t], [C32T[o]])
            TT(C32[:, o, :], C32[:, o, :], mr, ALU.subtract, [C32T[o], mr_t], [C32T[o]])
            ACT(Uap(o, st), C32[:, o, :], AF.Silu, [C32T[o], T_VECS], [UZT[o][st]],
                bias=VECS[:, C_CLB + o:C_CLB + o + 1], scale=VECS[:, C_CLG + o:C_CLG + o + 1])

        W.reserve = 2
        for g in range(2):
            iv, sv = W.acquire("m%din%d" % (hh, 4 + g))
            ic, scx = W.acquire("m%din%d" % (hh, 6 + g))
            for oc in range(4):
                o = g * 4 + oc
                for tl in range(2):
                    bA = PB.take()
                    bB = PB.take()
                    for k in range(8):
                        MM(PS[bA][:, :], wcol(sv, k, oc * 128), Hap(k, tl), k == 0, k == 7,
                           [RTile[sv], HT[k][tl]], [PT[bA]])
                    for k in range(8):
                        MM(PS[bB][:, :], wcol(scx, k, oc * 128), Hap(k, tl), k == 0, k == 7,
                           [RTile[scx], HT[k][tl]], [PT[bB]])
                    sgap, sgt = SGb.next()
                    ACT(sgap, PS[bB][:, :], AF.Sigmoid, [PT[bB], T_VECS], [sgt], bias=bin_col(3, o))
                    STT(Uap(o, tl), PS[bA][:, :], bin_col(2, o), sgap, ALU.add, ALU.mult,
                        [PT[bA], sgt, T_VECS], [UZT[o][tl]])
            W.release(iv)
            W.release(ic)
            if g == 0:
                build_dg(0)

        alias_into(m5_tiles, m2_tiles)
        if hh == 0:
            CP(HALO[:, :, 2:32], UZ[:, :, 32 + 994:32 + 1024], [UZT[o][1] for o in range(8)], [T_HALO])
        else:
            CP(UZ[:, :, 2:32], HALO[:, :, 2:32], [T_HALO], [ZHT])
        cbank = {}
        for idx in range(18):
            if idx < 16:
                st, o = ctiles[idx]
                if o == 0:
                    bmq[st] = (PB.take(hold=True), PB.take(hold=True))
                sl, DG = dg.pop(idx)
                b = PB.take(hold=True)
                cbank[idx] = b
                rds = [RTile[sl], UZT[o][st], (UZT[o][st - 1] if st > 0 else ZHT)]
                for k in range(31):
                    cc = 2 + st * 512 + k
                    MM(PS[b][:, :], DG[:, k, :], UZ[:, o, cc:cc + 512], k == 0, k == 30, rds, [PT[b]])
                W.give_raw(sl)
                if idx + 1 < 16:
                    build_dg(idx + 1)
            if 0 <= idx - 2 < 16:
                stats_mm(idx - 2)
                if ctiles[idx - 2] == (1, 7):
                    stat_chain(1)
            if 1 <= idx <= 16:
                j = idx - 1
                st, o = ctiles[j]
                b = cbank.pop(j)
                if st == 0:
                    norm_o(1, o)
                cvb = VECS[:, C_CVB + o:C_CVB + o + 1]
                ACT(C32[:, o, :], PS[b][:, :], AF.Identity, [PT[b], T_VECS], [C32T[o]], bias=cvb)
                cb, cbt = SQ.next()
                cq, cqt = SQ.next()
                ACT(cb, PS[b][:, :], AF.Identity, [PT[b], T_VECS], [cbt], bias=cvb)
                ACT(cq, PS[b][:, :], AF.Square, [PT[b], T_VECS], [cqt], bias=cvb)
                PB.drop(b)
                evs[j] = (cb, cbt, cq, cqt)
        stat_chain(0)
        for o in range(8):
            norm_o(0, o)
        W.reserve = 0
        W.pump()

        m6 = []
        for g in range(2):
            ig, sgx = W.acquire("m%din%d" % (hh, 10 + g))
            ib, sbx = W.acquire("m%dwb%d" % (hh, g))
            m6.append((ig, sgx, ib, sbx))
        for tl in (1, 0):
            for g in range(2):
                ig, sgx, ib, sbx = m6[g]
                for oc in range(4):
                    o = g * 4 + oc
                    bA = PB.take()
                    bB = PB.take()
                    for k in range(8):
                        MM(PS[bA][:, :], wcol(sgx, k, oc * 128), Hap(k, tl), k == 0, k == 7,
                           [RTile[sgx], HT[k][tl]], [PT[bA]])
                    for k in range(8):
                        MM(PS[bB][:, :], wcol(sbx, k, oc * 128), Uap(k, tl), k == 0, k == 7,
                           [RTile[sbx], UZT[k][tl]], [PT[bB]])
                    sgap, sgt = SGb.next()
                    ACT(sgap, PS[bA][:, :], AF.Sigmoid, [PT[bA], T_VECS], [sgt], bias=bin_col(5, o))
                    tm, tmt = TMPN.next()
                    TT(tm, PS[bB][:, :], sgap, ALU.mult, [PT[bB], sgt], [tmt])
                    TT(Hap(o, 2 + tl), Hap(o, 2 + tl), tm, ALU.add, [HT[o][2 + tl], tmt], [HT[o][2 + tl]])
        for ig, sgx, ib, sbx in m6:
            W.release(ig)
            W.release(ib)

        if pre_m7 is not None:
            pre_m7()

        ti = 0
        for g in range(2):
            io, so = W.acquire("m%dwo%d" % (hh, g))
            for oc in range(4):
                o = g * 4 + oc
                for tl in range(2):
                    if ti in m7_hooks:
                        m7_hooks[ti]()
                    ti += 1
                    b = PB.take()
                    for k in range(8):
                        MM(PS[b][:, :], wcol(so, k, oc * 128), Hap(k, 2 + tl), k == 0, k == 7,
                           [RTile[so], HT[k][2 + tl]], [PT[b]])
                    tt = 2 * hh + tl
                    STT(Xap(o, tt), PS[b][:, :], MOD[:, 40 + o:41 + o], Xap(o, tt), ALU.mult, ALU.add,
                        [PT[b], XT[o][tt], MODT[1]], [XT[o][tt]])
            W.release(io)
        return [t for a in UZT for t in a] + [ZHT] + SGb.tiles + m2_tiles + m5_tiles

    outv = d_out.rearrange("(k p) t -> p k t", p=128)
    T_OUT = Tile()

    def final_tile(tt):
        r, rt = stats_tile(tt)
        for k in range(8):
            STT(Xap(k, tt), Xap(k, tt), VECS[:, C_NFIN + k:C_NFIN + k + 1], r, ALU.mult, ALU.mult,
                [XT[k][tt], rt, T_VECS], [XT[k][tt]])
        DMA("sp", outv[:, :, tt * 512:(tt + 1) * 512], X[:, :, tt * 512:(tt + 1) * 512], "out",
            [XT[k][tt] for k in range(8)], [T_OUT])

    r0 = stats_tile(0)
    ada_phase(0, range(4))
    apply_tile(0, r0[0], r0[1], 0, hmap_full)
    norm_tile(1, 0, hmap_full)
    norm_tile(2, 0, hmap_full)
    norm_tile(3, 0, hmap_full)

    h1 = {}

    def hook1(tt):
        if tt == 1:
            norm_tile(0, 1, hmap_half(0))
        elif tt == 2:
            h1[1] = stats_tile(1)

    def after_pro0():
        apply_tile(1, h1[1][0], h1[1][1], 1, hmap_half(0))

    ffn_phase("f1", 0, pre_dn7=lambda: ada_phase(1), dn7_hook=hook1,
              pre_dn0=lambda: ada_phase(0, (4, 5)), gt=DERT[3])

    hst = {}

    def mk_hooks(tts, g, hmap):
        def st_(tt):
            hst[tt] = stats_tile(tt)

        def ap_(tt):
            r, rt = hst.pop(tt)
            apply_tile(tt, r, rt, g, hmap)
        return {0: lambda: st_(tts[0]), 4: lambda: ap_(tts[0]), 8: lambda: st_(tts[1]), 12: lambda: ap_(tts[1])}

    mixer_half(0, True, None, mk_hooks((2, 3), 1, hmap_half(1)), after_pro0)
    mtiles = mixer_half(1, False, lambda: ada_phase(2), mk_hooks((0, 1), 2, hmap_full))
    ffn_phase.prev_tiles = mtiles
    norm_tile(2, 2, hmap_full)
    norm_tile(3, 2, hmap_full)

    def hook2(tt):
        if tt >= 1:
            final_tile(tt - 1)

    ffn_phase("f2", 2, dn7_hook=hook2)
    final_tile(3)
    T_OUT.w = {"out": S.dsem["out"]}
    S.emit("sp", None, [T_OUT], ())

    assert W.cur == len(blocks) and W.nxt == len(blocks), (W.cur, W.nxt, len(blocks))

    sig = {e: set() for e in ("pe", "act", "dve", "pool")}
    for e in S.ENG:
        for fn, waits, tok in S.q[e]:
            for k, v in waits:
                if k in sig:
                    sig[k].add(v)
    rank = {}
    for e, st_ in sig.items():
        rank[e] = {v: i + 1 for i, v in enumerate(sorted(st_))}

    def replay(ename, handle):
        for fn, waits, tok in S.q[ename]:
            for k, v in waits:
                if k in sem_eng:
                    handle.wait_ge(sem_eng[k], rank[k][v])
                else:
                    handle.wait_ge(sem_dma[k], v)
            if fn is None:
                continue
            inst = fn(handle)
            if tok[0] in sem_dma:
                inst.then_inc(sem_dma[tok[0]], 16)
            elif tok[1] in sig[tok[0]]:
                inst.then_inc(sem_eng[tok[0]], 1)

    with nc.Block() as block:
        @block.sync
        def _(e):
            replay("sp", e)

        @block.gpsimd
        def _(e):
            replay("pool", e)

        @block.scalar
        def _(e):
            replay("act", e)

        @block.vector
        def _(e):
            replay("dve", e)

        @block.tensor
        def _(e):
            replay("pe", e)

    es.close()
    return nc


def prep_inputs(inp):
    g = lambda n: np.asarray(inp[n], dtype=np.float32)
    x = g("x")
    c = g("c")
    vecs = np.zeros((128, NV), np.float32)

    def put(col, v):
        v = np.asarray(v, np.float32).reshape(-1, 128).T
        vecs[:, col:col + v.shape[1]] = v

    put(C_NF1, g("norm_ffn1")[0])
    put(C_NMIX, g("norm_mix")[0])
    put(C_NF2, g("norm_ffn2")[0])
    put(C_NFIN, g("norm_final"))
    put(C_BIN, g("mix_b_in")[0])
    put(C_SLG, g("sgu_ln_g")[0])
    put(C_SLB, g("sgu_ln_b")[0])
    put(C_CVB, g("conv_b")[0])
    put(C_CLG, g("conv_ln_g")[0])
    put(C_CLB, g("conv_ln_b")[0])
    put(C_ADAB, g("ada_b")[0])
    cw = g("conv_w")[0]
    for o in range(8):
        vecs[:, C_CW + o * 31:C_CW + (o + 1) * 31] = cw[:, o * 128:(o + 1) * 128].T
    wsT = np.ascontiguousarray(g("sgu_w_s")[0].transpose(2, 0, 1))
    s_idx = np.arange(128)
    mask = (s_idx[:, None] <= s_idx[None, :]).astype(np.float32)
    ident = np.eye(128, dtype=np.float32)
    rows = np.concatenate([g("mix_b_in")[0][1024:2048], g("sgu_b_s")[0].reshape(-1), g("sgu_ln_b")[0]])[None, :]
    rows = np.ascontiguousarray(rows, dtype=np.float32)
    shared = {
        "vecs": vecs, "wsT": wsT, "mask": mask, "ident": ident, "rows": rows,
        "ada_w": np.ascontiguousarray(g("ada_w")[0]),
        "f1g": np.ascontiguousarray(g("ffn1_w_gate")[0]), "f1u": np.ascontiguousarray(g("ffn1_w_up")[0]),
        "f1d": np.ascontiguousarray(g("ffn1_w_down")[0]),
        "f2g": np.ascontiguousarray(g("ffn2_w_gate")[0]), "f2u": np.ascontiguousarray(g("ffn2_w_up")[0]),
        "f2d": np.ascontiguousarray(g("ffn2_w_down")[0]),
        "w_in": np.ascontiguousarray(g("mix_w_in")[0]),
        "w_a": np.ascontiguousarray(g("w_branch_a")[0]), "w_b": np.ascontiguousarray(g("w_branch_b")[0]),
        "w_o": np.ascontiguousarray(g("w_out")[0]),
    }
    in_maps = []
    for b in range(8):
        m = dict(shared)
        m["xT"] = np.ascontiguousarray(x[b].T)
        m["cvec"] = np.ascontiguousarray(c[b].reshape(8, 128).T)
        in_maps.append(m)
    return in_maps


def kernel(**inputs):
    in_maps = prep_inputs(inputs)
    nc = build_program(STAGE if STAGE < 9 else 4)
    res = run_bass_kernel_spmd(nc, in_maps, core_ids=list(range(8)))
    out = np.stack([np.asarray(r["outT"], dtype=np.float32).T for r in res.results], axis=0)
    return np.ascontiguousarray(out)
```
